# Optimizing a Trainium2 kernel written in Bass

```python
import math
import jax, jax.numpy as jnp
from jax import lax
import numpy as np

D_MODEL = 2048
BATCH = 4
SEQ = 2048
DEPTH = 4

GRID_W = 64
CTX_LEN = 256

D_MIX = D_MODEL
MLA_V = 128
MLA_NOPE = 128
MLA_ROPE = 64
MLA_W = D_MIX // 2
MLA_HEADS = MLA_W // MLA_V
MLA_Q_RANK = 512
MLA_KV_RANK = 256
MLA_SCALE = (MLA_NOPE + MLA_ROPE) ** -0.5
MLA_QB = 128
LRU_W = D_MIX // 4
LRU_BLOCKS = 8
LRU_BW = LRU_W // LRU_BLOCKS
LRU_CONV = 4
LRU_C = 8.0
RET_HEADS = 4
RET_DH = 128
RET_W = RET_HEADS * RET_DH
RET_CHUNK = 128
RET_K_SCALE = RET_DH ** -0.5

ROPE_BASE = 10000.0
LN_EPS = 1e-5
RMS_EPS = 1e-6
ALPHA = (2 * DEPTH) ** 0.25
BETA = (8 * DEPTH) ** -0.25

SPLITS = (MLA_Q_RANK, MLA_KV_RANK, MLA_ROPE, MLA_W,
          LRU_W, LRU_W,
          RET_W, RET_W, RET_W, RET_W)
MIX_IN = sum(SPLITS)

kernel_name = 'hybrid_mla_rglru_retention_dit'


def layer_norm(x, g=None, b=None):
    xf = x.astype(jnp.float32)
    mu = jnp.mean(xf, axis=-1, keepdims=True)
    var = jnp.mean(jnp.square(xf - mu), axis=-1, keepdims=True)
    y = (xf - mu) * lax.rsqrt(var + LN_EPS)
    if g is not None:
        y = y * g.astype(jnp.float32) + b.astype(jnp.float32)
    return y.astype(x.dtype)


def rms_norm(x, g):
    xf = x.astype(jnp.float32)
    y = xf * lax.rsqrt(jnp.mean(jnp.square(xf), axis=-1, keepdims=True) + RMS_EPS)
    return (y * g.astype(jnp.float32)).astype(x.dtype)


def split_cols(u):
    offsets = np.cumsum(SPLITS)[:-1].tolist()
    return jnp.split(u, offsets, axis=-1)


def ada_mod(cond, w, b):
    m = jax.nn.silu(cond) @ w + b
    return jnp.split(m, 3, axis=-1)


def axial_rope_tables(rows, dim, dtype):
    row = jnp.repeat(jnp.arange(rows, dtype=jnp.float32), GRID_W)
    col = jnp.tile(jnp.arange(GRID_W, dtype=jnp.float32), rows)
    quarter = dim // 4
    inv = ROPE_BASE ** (-jnp.arange(quarter, dtype=jnp.float32) / quarter)
    ang = jnp.stack([row[:, None] * inv, col[:, None] * inv], axis=1)
    return jnp.cos(ang).astype(dtype), jnp.sin(ang).astype(dtype)


def apply_rope(x, cos, sin):
    shp = x.shape
    xr = x.reshape(shp[:-1] + (2, 2, shp[-1] // 4))
    if x.ndim == 4:
        cos, sin = cos[:, None], sin[:, None]
    x1, x2 = xr[..., 0, :], xr[..., 1, :]
    out = jnp.stack([x1 * cos - x2 * sin, x2 * cos + x1 * sin], axis=-2)
    return out.reshape(shp)


def mla_queries(q_lat, g, w_uq):
    B, T, _ = q_lat.shape
    q = (rms_norm(q_lat, g) @ w_uq).reshape(B, T, MLA_HEADS, MLA_NOPE + MLA_ROPE)
    return q[..., :MLA_NOPE], q[..., MLA_NOPE:]


def mla_kv(kv_lat, g, w_ukv):
    B, T, _ = kv_lat.shape
    kv = (rms_norm(kv_lat, g) @ w_ukv).reshape(B, T, MLA_HEADS, MLA_NOPE + MLA_V)
    return kv[..., :MLA_NOPE], kv[..., MLA_NOPE:]


def mla_scores(qn, qr, kn, kr):
    return jnp.einsum('bqhd,bkhd->bhqk', qn, kn) + jnp.einsum('bqhd,bkd->bhqk', qr, kr)


def softmax_attend(s, v):
    p = jax.nn.softmax(s.astype(jnp.float32) * MLA_SCALE, axis=-1).astype(v.dtype)
    return jnp.einsum('bhqk,bkhd->bqhd', p, v)


def mla_latent(qn, qr, kn_all, kr_all, v_all):
    B, T = qn.shape[:2]
    nb = T // MLA_QB

    def to_blocks(a):
        return a.reshape((B, nb, MLA_QB) + a.shape[2:]).swapaxes(0, 1)

    def block(args):
        bqn, bqr = args
        return softmax_attend(mla_scores(bqn, bqr, kn_all, kr_all), v_all)

    o = lax.map(block, (to_blocks(qn), to_blocks(qr)))
    return o.swapaxes(0, 1).reshape(B, T, MLA_W)


def mla_mixer(q_lat, kv_lat, k_rope, gate, q_lat_c, kv_lat_c, k_rope_c, gate_c,
              q_norm_g, kv_norm_g, w_uq, w_ukv, cos, sin, need_ctx_out):
    kn_c, v_c = mla_kv(kv_lat_c, kv_norm_g, w_ukv)
    kn, v = mla_kv(kv_lat, kv_norm_g, w_ukv)
    kr = apply_rope(k_rope, cos, sin)
    qn, qr = mla_queries(q_lat, q_norm_g, w_uq)
    qr = apply_rope(qr, cos, sin)
    kn_all = jnp.concatenate([kn_c, kn], axis=1)
    kr_all = jnp.concatenate([k_rope_c, kr], axis=1)
    v_all = jnp.concatenate([v_c, v], axis=1)
    y = mla_latent(qn, qr, kn_all, kr_all, v_all) * jax.nn.silu(gate)
    yc = None
    if need_ctx_out:
        B, L = q_lat_c.shape[:2]
        qn_c, qr_c = mla_queries(q_lat_c, q_norm_g, w_uq)
        o_c = softmax_attend(mla_scores(qn_c, qr_c, kn_c, k_rope_c), v_c)
        yc = o_c.reshape(B, L, MLA_W) * jax.nn.silu(gate_c)
    return y, yc


def short_conv(x, w, b):
    T = x.shape[1]
    left = LRU_CONV // 2
    xp = jnp.pad(x, ((0, 0), (left, LRU_CONV - 1 - left), (0, 0)))
    return sum(xp[:, k:k + T] * w[k] for k in range(LRU_CONV)) + b


def block_diag(x, w, b):
    B, T, _ = x.shape
    y = jnp.einsum('btgi,gij->btgj', x.reshape(B, T, LRU_BLOCKS, LRU_BW), w)
    return y.reshape(B, T, LRU_W) + b


def rglru_coeffs(x, w_r, b_r, w_i, b_i, lam):
    r = jax.nn.sigmoid(block_diag(x, w_r, b_r))
    i = jax.nn.sigmoid(block_diag(x, w_i, b_i))
    log_a = -LRU_C * r * jax.nn.softplus(-lam)
    a = jnp.exp(log_a)
    return a, jnp.sqrt(-jnp.expm1(2.0 * log_a)) * (i * x)


def _lin_combine(e1, e2):
    a1, b1 = e1
    a2, b2 = e2
    return a1 * a2, a2 * b1 + b2


def linear_scan(a, b, h0, reverse):
    if reverse:
        b = b.at[:, -1].add(a[:, -1] * h0)
    else:
        b = b.at[:, 0].add(a[:, 0] * h0)
    _, h = lax.associative_scan(_lin_combine, (a, b), axis=1, reverse=reverse)
    return h


def rglru_mixer(x_lat, gate, x_ctx, gate_c, conv_w, conv_b, w_r, b_r, w_i, b_i, lam, need_ctx_out):
    xl = short_conv(x_lat, conv_w, conv_b)
    xc = short_conv(x_ctx, conv_w, conv_b)
    outs, outs_c = [], []
    for d, reverse in ((0, False), (1, True)):
        a_c, b_c = rglru_coeffs(xc, w_r[d], b_r[d], w_i[d], b_i[d], lam[d])
        h_c = linear_scan(a_c, b_c, jnp.zeros_like(xc[:, 0]), reverse)
        h_c_last = h_c[:, 0] if reverse else h_c[:, -1]
        a, b = rglru_coeffs(xl, w_r[d], b_r[d], w_i[d], b_i[d], lam[d])
        outs.append(linear_scan(a, b, h_c_last, reverse))
        outs_c.append(h_c)
    y = (outs[0] + outs[1]) * jax.nn.silu(gate)
    yc = (outs_c[0] + outs_c[1]) * jax.nn.silu(gate_c) if need_ctx_out else None
    return y, yc


def retention_dir(q, k, v, log_g, r0, include_diag):
    B, T, H, dk = q.shape
    nc = T // RET_CHUNK
    qc = q.reshape(B, nc, RET_CHUNK, H, dk)
    kc = k.reshape(B, nc, RET_CHUNK, H, dk)
    vc = v.reshape(B, nc, RET_CHUNK, H, v.shape[-1])
    idx = jnp.arange(RET_CHUNK, dtype=jnp.float32)
    diff = idx[:, None] - idx[None, :]
    mask = diff >= 0 if include_diag else diff > 0
    decay = jnp.where(mask[None], jnp.exp(jnp.where(mask, diff, 0.0)[None] * log_g[:, None, None]), 0.0)
    decay = decay.astype(q.dtype)
    s = jnp.einsum('bcqhd,bckhd->bchqk', qc, kc) * decay
    inner = jnp.einsum('bchqk,bckhe->bcqhe', s, vc)
    zeta = jnp.exp((RET_CHUNK - 1 - idx)[:, None] * log_g[None]).astype(q.dtype)
    xi = jnp.exp((idx + 1)[:, None] * log_g[None]).astype(q.dtype)
    g_chunk = jnp.exp(RET_CHUNK * log_g).astype(q.dtype)[:, None, None]
    kv_chunk = jnp.einsum('bckhd,kh,bckhe->bchde', kc, zeta, vc)

    def step(r, kv):
        return g_chunk * r + kv, r

    _, r_prev = lax.scan(step, r0, kv_chunk.swapaxes(0, 1))
    cross = jnp.einsum('bcqhd,qh,cbhde->bcqhe', qc, xi, r_prev)
    return (inner + cross).reshape(B, T, H, v.shape[-1])


def retention_ctx_state(k, v, log_g, reverse):
    L = k.shape[1]
    pos = jnp.arange(L, dtype=jnp.float32)
    expo = pos if reverse else (L - 1 - pos)
    w = jnp.exp(expo[:, None] * log_g[None]).astype(k.dtype)
    return jnp.einsum('blhd,lh,blhe->bhde', k, w, v)


def bidir_retention(q, k, v, log_g, r_f, r_b):
    flip = lambda a: jnp.flip(a, axis=1)
    fwd = retention_dir(q, k, v, log_g[0], r_f, True)
    bwd = flip(retention_dir(flip(q), flip(k), flip(v), log_g[1], r_b, False))
    return fwd + bwd


def head_norm(o):
    B, T = o.shape[:2]
    return layer_norm(o).reshape(B, T, RET_W)


def retention_mixer(q, k, v, gate, q_c, k_c, v_c, gate_c, decay_raw, cos, sin, need_ctx_out):
    log_g = jax.nn.log_sigmoid(decay_raw.astype(jnp.float32))
    heads = lambda a: a.reshape(a.shape[0], a.shape[1], RET_HEADS, RET_DH)
    q = apply_rope(heads(q), cos, sin)
    k = apply_rope(heads(k), cos, sin) * RET_K_SCALE
    v = heads(v)
    kc = heads(k_c) * RET_K_SCALE
    vc = heads(v_c)
    r_f = retention_ctx_state(kc, vc, log_g[0], False)
    r_b = retention_ctx_state(kc, vc, log_g[1], True)
    y = head_norm(bidir_retention(q, k, v, log_g, r_f, r_b)) * jax.nn.silu(gate)
    yc = None
    if need_ctx_out:
        zeros = jnp.zeros_like(r_f)
        yc = head_norm(bidir_retention(heads(q_c), kc, vc, log_g, zeros, zeros)) * jax.nn.silu(gate_c)
    return y, yc


def setup_inputs(seed: int = 0) -> dict:
    key = jax.random.key(seed)
    ks = jax.random.split(key, 22)
    f32 = jnp.float32
    nrm = lambda k, shape, s: jax.random.normal(k, shape, f32) * s
    L = DEPTH
    u = jax.random.uniform(ks[17], (L, 2, LRU_W), f32, 0.9, 0.999)
    a0 = u ** (1.0 / LRU_C)
    lru_lambda = jnp.log(a0) - jnp.log1p(-a0)
    gamma0 = 1.0 - 2.0 ** (-5.0 - jnp.arange(RET_HEADS, dtype=f32))
    ret_decay = jnp.log(gamma0) - jnp.log1p(-gamma0) + nrm(ks[18], (L, 2, RET_HEADS), 0.05)
    return {
        'x': nrm(ks[0], (BATCH, SEQ, D_MODEL), 1.0),
        'c': nrm(ks[1], (BATCH, D_MODEL), 1.0),
        'ctx': nrm(ks[2], (BATCH, CTX_LEN, D_MODEL), 1.0),
        'c_ctx': nrm(ks[3], (D_MODEL,), 1.0),
        'w_ada': nrm(ks[4], (L, D_MODEL, 3 * D_MODEL), 0.5 * D_MODEL ** -0.5),
        'b_ada': nrm(ks[5], (L, 3 * D_MODEL), 0.02),
        'w_in': nrm(ks[6], (L, D_MODEL, MIX_IN), D_MODEL ** -0.5),
        'mla_q_norm_g': 1.0 + nrm(ks[7], (L, MLA_Q_RANK), 0.02),
        'mla_kv_norm_g': 1.0 + nrm(ks[8], (L, MLA_KV_RANK), 0.02),
        'mla_w_uq': nrm(ks[9], (L, MLA_Q_RANK, MLA_HEADS * (MLA_NOPE + MLA_ROPE)), MLA_Q_RANK ** -0.5),
        'mla_w_ukv': nrm(ks[10], (L, MLA_KV_RANK, MLA_HEADS * (MLA_NOPE + MLA_V)), MLA_KV_RANK ** -0.5),
        'lru_conv_w': nrm(ks[11], (L, LRU_CONV, LRU_W), LRU_CONV ** -0.5),
        'lru_conv_b': nrm(ks[12], (L, LRU_W), 0.02),
        'lru_w_r': nrm(ks[13], (L, 2, LRU_BLOCKS, LRU_BW, LRU_BW), LRU_BW ** -0.5),
        'lru_b_r': nrm(ks[14], (L, 2, LRU_W), 0.02),
        'lru_w_i': nrm(ks[15], (L, 2, LRU_BLOCKS, LRU_BW, LRU_BW), LRU_BW ** -0.5),
        'lru_b_i': nrm(ks[16], (L, 2, LRU_W), 0.02),
        'lru_lambda': lru_lambda,
        'ret_decay': ret_decay,
        'w_out': nrm(ks[19], (L, D_MIX, D_MODEL), BETA * D_MIX ** -0.5),
        'ln_g': 1.0 + nrm(ks[20], (L, D_MODEL), 0.02),
        'ln_b': nrm(ks[21], (L, D_MODEL), 0.02),
    }


def reference(x, c, ctx, c_ctx, w_ada, b_ada, w_in, mla_q_norm_g, mla_kv_norm_g, mla_w_uq, mla_w_ukv,
              lru_conv_w, lru_conv_b, lru_w_r, lru_b_r, lru_w_i, lru_b_i, lru_lambda, ret_decay,
              w_out, ln_g, ln_b):
    rows = x.shape[1] // GRID_W
    cos_m, sin_m = axial_rope_tables(rows, MLA_ROPE, x.dtype)
    cos_r, sin_r = axial_rope_tables(rows, RET_DH, x.dtype)
    h = layer_norm(x)
    hc = layer_norm(ctx)
    for l in range(DEPTH):
        need_ctx_out = l < DEPTH - 1
        sh, sc, gt = ada_mod(c, w_ada[l], b_ada[l])
        shc, scc, gtc = ada_mod(c_ctx, w_ada[l], b_ada[l])
        u = split_cols((h * (1.0 + sc[:, None]) + sh[:, None]) @ w_in[l])
        uc = split_cols((hc * (1.0 + scc) + shc) @ w_in[l])
        y_mla, yc_mla = mla_mixer(u[0], u[1], u[2], u[3], uc[0], uc[1], uc[2], uc[3],
                                  mla_q_norm_g[l], mla_kv_norm_g[l], mla_w_uq[l], mla_w_ukv[l],
                                  cos_m, sin_m, need_ctx_out)
        y_lru, yc_lru = rglru_mixer(u[4], u[5], uc[4], uc[5], lru_conv_w[l], lru_conv_b[l],
                                    lru_w_r[l], lru_b_r[l], lru_w_i[l], lru_b_i[l], lru_lambda[l],
                                    need_ctx_out)
        y_ret, yc_ret = retention_mixer(u[6], u[7], u[8], u[9], uc[6], uc[7], uc[8], uc[9],
                                        ret_decay[l], cos_r, sin_r, need_ctx_out)
        y = jnp.concatenate([y_mla, y_lru, y_ret], axis=-1) @ w_out[l]
        h_new = layer_norm(ALPHA * h + gt[:, None] * y, ln_g[l], ln_b[l])
        if need_ctx_out:
            yc = jnp.concatenate([yc_mla, yc_lru, yc_ret], axis=-1) @ w_out[l]
            hc = layer_norm(ALPHA * hc + gtc * yc, ln_g[l], ln_b[l])
        h = h_new
    return h
```

```python
import numpy as np
from contextlib import ExitStack
import concourse.bass as bass
import concourse.mybir as mybir
from concourse.bass_utils import run_bass_kernel_spmd

F32 = mybir.dt.float32
BF16 = mybir.dt.bfloat16
AF = mybir.ActivationFunctionType
ALU = mybir.AluOpType

D = 2048
T = 2048
L = 256
N = T + L
NL = 4
MIX_IN = 4928
ALPHA = float(8 ** 0.25)
LN_EPS = 1e-5
RMS_EPS = 1e-6
MLA_SCALE = float(192 ** -0.5)
RET_K_SCALE = float(128 ** -0.5)
BIG = 1.0e6
TBLK = [(0, 256), (256, 512), (768, 512), (1280, 512), (1792, 512)]


class Buf:
    __slots__ = ("name", "w", "r")

    def __init__(self, name="b"):
        self.name = name
        self.w = None
        self.r = {}


class Prog:
    ENG = ("pe", "act", "dve", "pool", "sp")

    def __init__(self, nc, ndma=12):
        self.nc = nc
        self.ops = {e: [] for e in self.ENG}
        self.cnt = {e: 0 for e in self.ENG}
        self.known = {e: {} for e in self.ENG}
        self.dma_cnt = {}
        self.dma_rr = {e: 0 for e in self.ENG}
        self.ndma = ndma
        self.stack = ExitStack()
        self.nalloc = 0
        self.sems = {}
        for e in ("pe", "act", "dve", "pool"):
            self.sems["E:" + e] = self.stack.enter_context(nc.semaphore("E_" + e))
        for q in ("sp", "pool", "act"):
            for j in range(ndma):
                self.sems[f"D:{q}:{j}"] = self.stack.enter_context(nc.semaphore(f"D_{q}_{j}"))

    def sbuf(self, shape, dtype, name=None):
        self.nalloc += 1
        return self.stack.enter_context(self.nc.sbuf_tensor(name or f"sb{self.nalloc}", list(shape), dtype))

    def _deps(self, eng, reads, writes):
        w = {}

        def add(ev):
            if ev is None:
                return
            s, v = ev
            if w.get(s, 0) < v:
                w[s] = v
        for b in reads:
            add(b.w)
        for b in writes:
            add(b.w)
            for s, v in b.r.items():
                add((s, v))
        out = []
        for s, v in w.items():
            if eng == "pe" and s == "E:pe":
                continue
            if self.known[eng].get(s, 0) >= v:
                continue
            self.known[eng][s] = v
            out.append((s, v))
        return out

    def _mark(self, ev, reads, writes):
        for b in writes:
            b.w = ev
            b.r = {}
        for b in reads:
            if b.r.get(ev[0], 0) < ev[1]:
                b.r[ev[0]] = ev[1]

    def op(self, eng, fn, reads=(), writes=(), inc=True):
        waits = self._deps(eng, reads, writes)
        s = "E:" + eng
        ev = (s, self.cnt[eng] + 1)
        if inc:
            self.cnt[eng] += 1
        self.ops[eng].append((waits, fn, (s, 1) if inc else None))
        self._mark(ev, reads, writes)
        return ev

    def dma(self, q, out_ap, in_ap, reads=(), writes=(), **kw):
        waits = self._deps(q, reads, writes)
        j = self.dma_rr[q]
        self.dma_rr[q] = (j + 1) % self.ndma
        s = f"D:{q}:{j}"
        c = self.dma_cnt.get(s, 0)
        if c > 0 and self.known[q].get(s, 0) < 16 * c:
            waits.append((s, 16 * c))
            self.known[q][s] = 16 * c
        self.dma_cnt[s] = c + 1
        ev = (s, 16 * (c + 1))
        self.ops[q].append((waits, (lambda e, o=out_ap, i=in_ap, k=kw: e.dma_start(out=o, in_=i, **k)), (s, 16)))
        self._mark(ev, reads, writes)
        return ev

    def barrier(self):
        tot = {}
        for e in self.ENG:
            if self.cnt[e]:
                tot["E:" + e] = self.cnt[e]
        for s, c in self.dma_cnt.items():
            tot[s] = 16 * c
        for e in self.ENG:
            waits = []
            for s, v in tot.items():
                if self.known[e].get(s, 0) >= v:
                    continue
                self.known[e][s] = v
                waits.append((s, v))
            if waits:
                self.ops[e].append((waits, None, None))

    def flush(self):
        nc = self.nc
        sems = self.sems
        with nc.Block() as block:
            def run(eo, lst):
                for waits, fn, inc in lst:
                    for s_, v in waits:
                        eo.wait_ge(sems[s_], v)
                    if fn is not None:
                        ins = fn(eo)
                        if inc:
                            ins.then_inc(sems[inc[0]], inc[1])

            if self.ops["pe"]:
                @block.tensor
                def _(e):
                    run(e, self.ops["pe"])
            if self.ops["act"]:
                @block.scalar
                def _(e):
                    run(e, self.ops["act"])
            if self.ops["dve"]:
                @block.vector
                def _(e):
                    run(e, self.ops["dve"])
            if self.ops["pool"]:
                @block.gpsimd
                def _(e):
                    run(e, self.ops["pool"])
            if self.ops["sp"]:
                @block.sync
                def _(e):
                    run(e, self.ops["sp"])
        self.ops = {e: [] for e in self.ENG}

    def phase_end(self):
        self.barrier()
        self.flush()

    def mm(self, out, lhsT, rhs, start, stop, reads, writes, inc=None):
        if inc is None:
            inc = stop
        return self.op("pe", lambda e: e.matmul(out, lhsT=lhsT, rhs=rhs, start=start, stop=stop),
                       reads=reads, writes=writes, inc=inc)

    def tr(self, out, in_, ident, reads, writes, inc=True):
        return self.op("pe", lambda e: e.transpose(out=out, in_=in_, identity=ident), reads=reads, writes=writes, inc=inc)

    def act(self, out, in_, func, reads, writes, bias=None, scale=None):
        kw = {}
        if bias is not None:
            kw["bias"] = bias
        if scale is not None:
            kw["scale"] = scale
        return self.op("act", lambda e: e.activation(out=out, in_=in_, func=func, **kw), reads=reads, writes=writes)

    def cp(self, eng, out, in_, reads, writes):
        if eng == "act":
            return self.op("act", lambda e: e.copy(out=out, in_=in_), reads=reads, writes=writes)
        return self.op(eng, lambda e: e.tensor_copy(out=out, in_=in_), reads=reads, writes=writes)

    def tt(self, eng, out, a, b, op, reads, writes):
        return self.op(eng, lambda e: e.tensor_tensor(out=out, in0=a, in1=b, op=op), reads=reads, writes=writes)

    def ts(self, eng, out, a, s1, s2, op0, op1, reads, writes):
        if s2 is None:
            return self.op(eng, lambda e: e.tensor_single_scalar(out=out, in_=a, scalar=s1, op=op0), reads=reads, writes=writes)
        return self.op(eng, lambda e: e.tensor_scalar(out=out, in0=a, scalar1=s1, scalar2=s2, op0=op0, op1=op1),
                       reads=reads, writes=writes)

    def stt(self, eng, out, a, s, b, op0, op1, reads, writes):
        return self.op(eng, lambda e: e.scalar_tensor_tensor(out=out, in0=a, scalar=s, in1=b, op0=op0, op1=op1),
                       reads=reads, writes=writes)

    def memset(self, eng, ap, val, writes):
        return self.op(eng, lambda e: e.memset(ap, val), writes=writes)


def host_consts():
    c = {}
    c["ident"] = np.eye(128, dtype=np.float32)

    def perm(dim):
        q = dim // 4
        P = np.zeros((dim, dim), np.float32)
        for a in range(2):
            for s in range(2):
                for f in range(q):
                    i = a * 2 * q + s * q + f
                    j = a * 2 * q + (1 - s) * q + f
                    P[j, i] = 1.0
        return P
    c["perm64"] = perm(64)
    c["perm128"] = perm(128)
    sel = np.zeros((2, 2, 128), np.float32)
    sel[0, 0, :] = 1.0
    sel[1, 1, :] = 1.0
    c["sel"] = sel.reshape(2, 256)

    def rope(dim, scale):
        rows = T // 64
        row = np.repeat(np.arange(rows, dtype=np.float32), 64)
        col = np.tile(np.arange(64, dtype=np.float32), rows)
        q = dim // 4
        inv = (np.float32(10000.0) ** (-np.arange(q, dtype=np.float32) / np.float32(q))).astype(np.float32)
        ang = np.stack([row[:, None] * inv, col[:, None] * inv], axis=1).astype(np.float32)
        cos = np.cos(ang).astype(np.float32)
        sin = np.sin(ang).astype(np.float32)
        C = np.zeros((dim, T), np.float32)
        S = np.zeros((dim, T), np.float32)
        for a in range(2):
            for s in range(2):
                i0 = a * 2 * q + s * q
                C[i0:i0 + q, :] = cos[:, a, :].T
                S[i0:i0 + q, :] = (sin[:, a, :].T) * (-1.0 if s == 0 else 1.0)
        return (C * np.float32(scale)).astype(np.float32), (S * np.float32(scale)).astype(np.float32)
    c["ropeM_C"], c["ropeM_S"] = rope(64, 1.0)
    c["ropeR_C"], c["ropeR_S"] = rope(128, 1.0)
    k = np.arange(128, dtype=np.float32)[:, None]
    q = np.arange(128, dtype=np.float32)[None, :]
    EDf = np.where(q >= k, q - k, BIG)
    EDb = np.where(k > q, k - q, BIG)
    EXIf = np.broadcast_to(q + 1.0, (128, 128))
    EXIb = np.broadcast_to(128.0 - q, (128, 128))
    EZf = 127.0 - k
    EZb = k
    EWf = np.concatenate([255.0 - k, 255.0 - (128.0 + k)], axis=1)
    EWb = np.concatenate([k, 128.0 + k], axis=1)
    c128 = np.full((128, 1), 128.0, np.float32)
    c["rtab"] = np.concatenate([EDf, EDb, EXIf, EXIb, EZf, EZb, EWf, EWb, c128], axis=1).astype(np.float32)
    return c


RT_W = 128 * 4 + 7

W_NAMES = [("w_ada", [NL, D, 6144]), ("b_ada", [NL, 6144]), ("w_in", [NL, D, MIX_IN]),
           ("mla_q_norm_g", [NL, 512]), ("mla_kv_norm_g", [NL, 256]),
           ("mla_w_uq", [NL, 512, 1536]), ("mla_w_ukv", [NL, 256, 2048]),
           ("lru_conv_w", [NL, 4, 512]), ("lru_conv_b", [NL, 512]),
           ("lru_w_r", [NL, 2, 8, 64, 64]), ("lru_b_r", [NL, 2, 512]),
           ("lru_w_i", [NL, 2, 8, 64, 64]), ("lru_b_i", [NL, 2, 512]),
           ("lru_lambda", [NL, 2, 512]), ("ret_decay", [NL, 8]),
           ("w_out", [NL, D, D]), ("ln_g", [NL, D]), ("ln_b", [NL, D])]
C_NAMES = [("ident", [128, 128]), ("perm64", [64, 64]), ("perm128", [128, 128]), ("sel", [2, 256]),
           ("ropeM_C", [64, T]), ("ropeM_S", [64, T]), ("ropeR_C", [128, T]), ("ropeR_S", [128, T]),
           ("rtab", [128, RT_W])]


def build(nl=NL, dbg=(), stop_after=None):
    nc = bass.Bass("TRN2", target_bir_lowering=False)
    p = Prog(nc)
    uid = [0]

    def SBT(name, shape, dt):
        uid[0] += 1
        return nc.sbuf_tensor(f"{name}_{uid[0]}", list(shape), dt)

    def PST(name, shape, dt):
        uid[0] += 1
        return nc.psum_tensor(f"{name}_{uid[0]}", list(shape), dt)
    I = {}
    I["x"] = nc.dram_tensor("x", [T, D], F32, kind="ExternalInput").ap()
    I["ctx"] = nc.dram_tensor("ctx", [L, D], F32, kind="ExternalInput").ap()
    I["c2"] = nc.dram_tensor("c2", [2, D], F32, kind="ExternalInput").ap()
    for n_, shp in W_NAMES + C_NAMES:
        I[n_] = nc.dram_tensor(n_, shp, F32, kind="ExternalInput").ap()
    out_d = nc.dram_tensor("out", [T, D], F32, kind="ExternalOutput").ap()

    def scr(name, shape, dt):
        kind = "ExternalOutput" if name in dbg else "Internal"
        return nc.dram_tensor(name, list(shape), dt, kind=kind).ap()
    h_d = scr("h_d", [N, D], F32)
    uT_d = scr("uT_d", [MIX_IN, N], BF16)
    vtok_d = scr("vtok_d", [N, 512], BF16)
    qnT_d = scr("qnT_d", [8, 128, N], BF16)
    qrT_d = scr("qrT_d", [8, 64, N], BF16)
    knT_d = scr("knT_d", [8, 128, N], BF16)
    krT_d = scr("krT_d", [64, N], BF16)
    V_d = scr("V_d", [N, 1024], BF16)
    yT_d = scr("yT_d", [D, N], BF16)
    B_h, B_u, B_v, B_q, B_k, B_V, B_y = (Buf() for _ in range(7))

    identF = p.sbuf([128, 128], F32)
    identB = p.sbuf([128, 128], BF16)
    onesB = p.sbuf([128, 128], BF16)
    onesF = p.sbuf([128, 128], F32)
    perm64 = p.sbuf([64, 64], F32)
    perm128 = p.sbuf([128, 128], F32)
    selT = p.sbuf([2, 256], F32)
    epsT = p.sbuf([128, 4], F32)
    cT = p.sbuf([128, 16, 2], BF16)
    modT = p.sbuf([128, 32, 2], F32)
    gt_bc = p.sbuf([128, 2, D], F32)
    lng_bc = p.sbuf([128, D], F32)
    lnb_bc = p.sbuf([128, D], F32)
    rtab = p.sbuf([128, RT_W], F32)
    BC = Buf("const")
    Bmod = Buf("mod")

    ps_ctx = ExitStack()

    def psum(shape, dt=F32):
        p.nalloc += 1
        return ps_ctx.enter_context(PST(f"ps{p.nalloc}", list(shape), dt))

    with ExitStack() as es:
        cF = es.enter_context(SBT("cF", [128, 16, 2], F32))
        p.dma("sp", identF[:], I["ident"][:, :], writes=[BC])
        p.dma("sp", perm64[:], I["perm64"][:, :], writes=[BC])
        p.dma("sp", perm128[:], I["perm128"][:, :], writes=[BC])
        p.dma("sp", selT[:], I["sel"][:, :], writes=[BC])
        p.dma("sp", rtab[:], I["rtab"][:, :], writes=[BC])
        Bc = Buf()
        for r in range(2):
            p.dma("sp", cF[:, :, r], I["c2"][r, :].rearrange("(k p) -> p k", p=128), writes=[Bc], allow_slow_non_contiguous=True)
        p.memset("dve", onesB[:], 1.0, [BC])
        p.memset("dve", onesF[:], 1.0 / 128.0, [BC])
        p.memset("dve", epsT[:, 0:1], LN_EPS, [BC])
        p.memset("dve", epsT[:, 1:2], RMS_EPS, [BC])
        p.memset("dve", epsT[:, 2:3], 1.0, [BC])
        p.memset("dve", epsT[:, 3:4], 0.0, [BC])
        p.cp("dve", identB[:], identF[:], [BC], [BC])
        p.act(cT[:], cF[:], AF.Silu, [Bc], [BC])
        p.phase_end()

    def phase_ada(l):
        with ExitStack() as es, ExitStack() as pes:
            wt = [es.enter_context(SBT(f"adaw{i}", [128, 16, 512], BF16)) for i in range(2)]
            Bw = [Buf(), Buf()]
            badaT = es.enter_context(SBT("badaT", [128, 32], F32))
            gtrow = es.enter_context(SBT("gtrow", [2, D], F32))
            brow = es.enter_context(SBT("brow", [2, D], F32))
            ps_f = pes.enter_context(PST("ada_f", [128, 32, 2], F32))
            ps_r = [pes.enter_context(PST(f"ada_r{i}", [2, 512], F32)) for i in range(2)]
            ps_b = [pes.enter_context(PST(f"ada_b{i}", [128, 512], F32)) for i in range(2)]
            Bpf, Bpr, Bpb = Buf(), [Buf(), Buf()], [Buf(), Buf()]
            Bb, Bg = Buf(), Buf()
            p.dma("sp", badaT[:], I["b_ada"][l, 0:4096].rearrange("(j p) -> p j", p=128), writes=[Bb],
                  allow_slow_non_contiguous=True)
            p.dma("sp", brow[:], I["b_ada"][l:l + 1, 4096:6144].partition_broadcast(2), writes=[Bb])
            p.dma("sp", lng_bc[:], I["ln_g"][l:l + 1, :].partition_broadcast(128), writes=[Bmod])
            p.dma("sp", lnb_bc[:], I["ln_b"][l:l + 1, :].partition_broadcast(128), writes=[Bmod])
            for g in range(12):
                w = wt[g % 2]
                bw = Bw[g % 2]
                p.dma("pool", w[:], I["w_ada"][l, :, g * 512:(g + 1) * 512].rearrange("(k p) c -> p k c", p=128),
                      writes=[bw])
                if g < 8:
                    for cc in range(4):
                        j = g * 4 + cc
                        for k in range(16):
                            p.mm(ps_f[:, j, :], w[:, k, cc * 128:(cc + 1) * 128], cT[:, k, :], k == 0, k == 15,
                                 [bw, BC], [Bpf])
                else:
                    gg = g - 8
                    pr = ps_r[gg % 2]
                    for k in range(16):
                        p.mm(pr[:, :], cT[:, k, :], w[:, k, :], k == 0, k == 15, [bw, BC], [Bpr[gg % 2]])
                    p.tt("dve", gtrow[:, gg * 512:(gg + 1) * 512], pr[:, :], brow[:, gg * 512:(gg + 1) * 512], ALU.add,
                         [Bpr[gg % 2], Bb], [Bg])
            for r in range(2):
                p.tt("dve", modT[:, :, r], ps_f[:, :, r], badaT[:, :], ALU.add, [Bpf, Bb], [Bmod])
            p.ts("dve", modT[:, 16:32, :], modT[:, 16:32, :], 1.0, None, ALU.add, ALU.bypass, [Bmod], [Bmod])
            i = 0
            for r in range(2):
                for gg in range(4):
                    pb = ps_b[i % 2]
                    p.mm(pb[:, :], selT[:, r * 128:(r + 1) * 128], gtrow[:, gg * 512:(gg + 1) * 512], True, True,
                         [Bg, BC], [Bpb[i % 2]])
                    p.cp("act", gt_bc[:, r, gg * 512:(gg + 1) * 512], pb[:, :], [Bpb[i % 2]], [Bmod])
                    i += 1
            p.phase_end()

    def phase_AB(l):
        with ExitStack() as es, ExitStack() as pes:
            xmT = es.enter_context(SBT("xmT", [128, 16, N], BF16))
            Bx = [Buf() for _ in range(9)]
            hb = [es.enter_context(SBT(f"hblk{i}", [128, 2, D], F32)) for i in range(2)]
            Bhb = [Buf(), Buf()]
            st = es.enter_context(SBT("lnst", [128, 2, 4, 6], F32))
            mv = es.enter_context(SBT("lnmv", [128, 2, 4], F32))
            Bst = Buf()
            pT = [pes.enter_context(PST(f"pT{i}", [128, 256], F32)) for i in range(3)]
            BpT = [Buf() for _ in range(3)]
            ti = 0
            for u in range(9):
                h = hb[u % 2]
                bh = Bhb[u % 2]
                r = 1 if u == 0 else 0
                t0 = u * 256
                if l == 0:
                    src = I["ctx"] if u == 0 else I["x"][(u - 1) * 256:u * 256, :]
                    p.dma("sp", h[:], src.rearrange("(a p) d -> p a d", p=128), writes=[bh])
                    for a in range(2):
                        for c4 in range(4):
                            p.op("dve", lambda e, o=st[:, a, c4, :], i_=h[:, a, c4 * 512:(c4 + 1) * 512]: e.bn_stats(out=o, in_=i_),
                                 reads=[bh], writes=[Bst])
                        p.op("dve", lambda e, o=mv[:, a, 0:2], i_=st[:, a, :, :].rearrange("p c s -> p (c s)"): e.bn_aggr(out=o, in_=i_),
                             reads=[Bst], writes=[Bst])
                        p.act(mv[:, a, 2:3], mv[:, a, 1:2], AF.Sqrt, [Bst, BC], [Bst], bias=epsT[:, 0:1], scale=1.0)
                        p.op("dve", lambda e, o=mv[:, a, 2:3]: e.reciprocal(out=o, in_=o), reads=[Bst], writes=[Bst])
                        p.stt("dve", mv[:, a, 3:4], mv[:, a, 0:1], -1.0, mv[:, a, 2:3], ALU.mult, ALU.mult, [Bst], [Bst])
                        p.act(h[:, a, :], h[:, a, :], AF.Identity, [Bst, bh], [bh], bias=mv[:, a, 3:4], scale=mv[:, a, 2:3])
                    p.dma("sp", h_d[t0:t0 + 256, :].rearrange("(a p) d -> p a d", p=128), h[:], reads=[bh], writes=[B_h])
                else:
                    p.dma("sp", h[:], h_d[t0:t0 + 256, :].rearrange("(a p) d -> p a d", p=128), reads=[B_h], writes=[bh])
                for j in range(16):
                    pt = pT[ti % 3]
                    bp = BpT[ti % 3]
                    ti += 1
                    for a in range(2):
                        p.tr(pt[:, a * 128:(a + 1) * 128], h[:, a, j * 128:(j + 1) * 128], identF[:], [bh, BC], [bp],
                             inc=(a == 1))
                    p.act(xmT[:, j, t0:t0 + 256], pt[:, :], AF.Identity, [bp, Bmod], [Bx[u]],
                          bias=modT[:, j, r:r + 1], scale=modT[:, 16 + j, r:r + 1])
            wt = [es.enter_context(SBT(f"winw{i}", [128, 16, 512], BF16)) for i in range(2)]
            Bw = [Buf(), Buf()]
            ost = [es.enter_context(SBT(f"ost{i}", [128, N], BF16)) for i in range(3)]
            Bo = [Buf() for _ in range(3)]
            pm = [pes.enter_context(PST(f"pm{i}", [128, 512], F32)) for i in range(4)]
            Bpm = [Buf() for _ in range(4)]
            groups = [(0, 512), (512, 320)] + [(832 + 512 * i, 512) for i in range(8)]
            pi = 0
            oi = 0
            ev = 0
            for gi, (c0, wd) in enumerate(groups):
                w = wt[gi % 2]
                bw = Bw[gi % 2]
                p.dma("pool", w[:, :, 0:wd], I["w_in"][l, :, c0:c0 + wd].rearrange("(k p) c -> p k c", p=128), writes=[bw])
                if c0 == 3904:
                    for tile in range(18):
                        pmm = pm[pi % 4]
                        bpm = Bpm[pi % 4]
                        pi += 1
                        for k in range(16):
                            p.mm(pmm[:, :], xmT[:, k, tile * 128:(tile + 1) * 128], w[:, k, :], k == 0, k == 15,
                                 [bw, Bx[tile // 2]], [bpm])
                        o = ost[oi % 3]
                        bo = Bo[oi % 3]
                        oi += 1
                        p.cp("act" if ev % 2 else "dve", o[:, 0:512], pmm[:, :], [bpm], [bo])
                        ev += 1
                        p.dma("sp", vtok_d[tile * 128:(tile + 1) * 128, :], o[:, 0:512], reads=[bo], writes=[B_v])
                    continue
                nch = (wd + 127) // 128
                for cc in range(nch):
                    cw = min(128, wd - cc * 128)
                    col = c0 + cc * 128
                    is_gate = (832 <= col < 1856) or (2368 <= col < 2880) or (4416 <= col)
                    o = ost[oi % 3]
                    bo = Bo[oi % 3]
                    oi += 1
                    for bi, (t0, n) in enumerate(TBLK):
                        pmm = pm[pi % 4]
                        bpm = Bpm[pi % 4]
                        pi += 1
                        rd = [bw] + ([Bx[0]] if bi == 0 else [Bx[2 * bi - 1], Bx[2 * bi]])
                        for k in range(16):
                            p.mm(pmm[0:cw, 0:n], w[:, k, cc * 128:cc * 128 + cw], xmT[:, k, t0:t0 + n], k == 0, k == 15,
                                 rd, [bpm])
                        if is_gate:
                            p.act(o[0:cw, t0:t0 + n], pmm[0:cw, 0:n], AF.Silu, [bpm], [bo])
                        else:
                            p.cp("dve", o[0:cw, t0:t0 + n], pmm[0:cw, 0:n], [bpm], [bo])
                    p.dma("sp", uT_d[col:col + cw, :], o[0:cw, :], reads=[bo], writes=[B_u])
            p.phase_end()

    def phase_C(l):
        ctxq = l < nl - 1
        with ExitStack() as es, ExitStack() as pes:
            wuq = es.enter_context(SBT("wuq", [128, 4, 1536], BF16))
            wk = es.enter_context(SBT("wk", [128, 2, 8, 128], BF16))
            wv = es.enter_context(SBT("wv", [128, 2, 8, 128], BF16))
            gq = es.enter_context(SBT("gq", [128, 6], F32))
            Bw = Buf()
            p.dma("pool", wuq[:], I["mla_w_uq"][l].rearrange("(k p) c -> p k c", p=128), writes=[Bw])
            ukv = I["mla_w_ukv"][l].rearrange("(k p) (h s d) -> p k h s d", p=128, h=8, s=2)
            for k in range(2):
                p.dma("pool", wk[:, k, :, :], ukv[:, k, :, 0, :], writes=[Bw])
                p.dma("pool", wv[:, k, :, :], ukv[:, k, :, 1, :], writes=[Bw])
            p.dma("sp", gq[:, 0:4], I["mla_q_norm_g"][l].rearrange("(k p) -> p k", p=128), writes=[Bw],
                  allow_slow_non_contiguous=True)
            p.dma("sp", gq[:, 4:6], I["mla_kv_norm_g"][l].rearrange("(k p) -> p k", p=128), writes=[Bw],
                  allow_slow_non_contiguous=True)
            lat = [es.enter_context(SBT(f"lat{i}", [128, 6, 512], BF16)) for i in range(2)]
            krb = [es.enter_context(SBT(f"krb{i}", [64, 512], BF16)) for i in range(2)]
            Blat = [Buf(), Buf()]
            sq = es.enter_context(SBT("sq", [128, 6, 512], BF16))
            Bsq = Buf()
            rstd = es.enter_context(SBT("rstd", [128, 2, 512], F32))
            Brs = Buf()
            ln = es.enter_context(SBT("lnq", [128, 6, 512], BF16))
            Bln = Buf()
            ropeC = [es.enter_context(SBT(f"rpc{i}", [64, 512], F32)) for i in range(2)]
            ropeS = [es.enter_context(SBT(f"rps{i}", [64, 512], F32)) for i in range(2)]
            Brope = [Buf(), Buf()]
            xr = [es.enter_context(SBT(f"xr{i}", [64, 512], F32)) for i in range(2)]
            t1 = [es.enter_context(SBT(f"t1{i}", [64, 512], F32)) for i in range(2)]
            Bxr = [Buf(), Buf()]
            Bt1 = [Buf(), Buf()]
            stg = [es.enter_context(SBT(f"stg{i}", [128, 1024], BF16)) for i in range(4)]
            Bstg = [Buf() for _ in range(4)]
            pss = [pes.enter_context(PST(f"pss{i}", [128, 512], F32)) for i in range(2)]
            Bpss = [Buf(), Buf()]
            pq = [pes.enter_context(PST(f"pq{i}", [128, 512], F32)) for i in range(4)]
            Bpq = [Buf() for _ in range(4)]
            pp = [pes.enter_context(PST(f"pp{i}", [64, 512], F32)) for i in range(2)]
            Bpp = [Buf(), Buf()]
            qi = 0
            si = 0
            ri = 0
            ev = 0

            def rope64(src_ps, src_reads, dst, dst_buf, rb, n):
                nonlocal ri
                x_ = xr[ri % 2]
                bx_ = Bxr[ri % 2]
                t_ = t1[ri % 2]
                bt_ = Bt1[ri % 2]
                pp_ = pp[ri % 2]
                bpp_ = Bpp[ri % 2]
                ri += 1
                p.cp("act", x_[:, 0:n], src_ps, src_reads, [bx_])
                p.mm(pp_[:, 0:n], perm64[:, :], x_[:, 0:n], True, True, [bx_, BC], [bpp_])
                p.tt("pool", t_[:, 0:n], x_[:, 0:n], ropeC[rb][:, 0:n], ALU.mult, [bx_, Brope[rb]], [bt_])
                p.tt("dve", x_[:, 0:n], pp_[:, 0:n], ropeS[rb][:, 0:n], ALU.mult, [bpp_, Brope[rb]], [bx_])
                p.tt("dve", dst, x_[:, 0:n], t_[:, 0:n], ALU.add, [bx_, bt_], [dst_buf])

            for bi, (t0, n) in enumerate(TBLK):
                la = lat[bi % 2]
                kr = krb[bi % 2]
                bl = Blat[bi % 2]
                p.dma("sp", la[:, :, 0:n], uT_d[0:768, t0:t0 + n].rearrange("(k p) t -> p k t", p=128), reads=[B_u], writes=[bl])
                p.dma("sp", kr[:, 0:n], uT_d[768:832, t0:t0 + n], reads=[B_u], writes=[bl])
                rb = bi % 2
                if bi > 0:
                    p.dma("sp", ropeC[rb][:, 0:n], I["ropeM_C"][:, t0 - L:t0 - L + n], writes=[Brope[rb]])
                    p.dma("sp", ropeS[rb][:, 0:n], I["ropeM_S"][:, t0 - L:t0 - L + n], writes=[Brope[rb]])
                p.tt("pool", sq[:, :, 0:n], la[:, :, 0:n], la[:, :, 0:n], ALU.mult, [bl], [Bsq])
                for which, (k0, k1, dim) in enumerate(((0, 4, 512.0), (4, 6, 256.0))):
                    ps_ = pss[which]
                    for k in range(k0, k1):
                        p.mm(ps_[:, 0:n], onesB[:, :], sq[:, k, 0:n], k == k0, k == k1 - 1, [Bsq, BC], [Bpss[which]])
                    p.act(rstd[:, which, 0:n], ps_[:, 0:n], AF.Sqrt, [Bpss[which], BC], [Brs], bias=epsT[:, 1:2], scale=1.0 / dim)
                    p.op("dve", lambda e, o=rstd[:, which, 0:n]: e.reciprocal(out=o, in_=o), reads=[Brs], writes=[Brs])
                    for k in range(k0, k1):
                        p.stt("dve", ln[:, k, 0:n], la[:, k, 0:n], gq[:, k:k + 1], rstd[:, which, 0:n], ALU.mult, ALU.mult,
                              [bl, Bw, Brs], [Bln])
                if bi > 0 or ctxq:
                    for h in range(8):
                        pq_ = pq[qi % 4]
                        bq_ = Bpq[qi % 4]
                        qi += 1
                        for k in range(4):
                            p.mm(pq_[:, 0:n], wuq[:, k, h * 192:h * 192 + 128], ln[:, k, 0:n], k == 0, k == 3, [Bw, Bln], [bq_])
                        s_ = stg[si % 4]
                        bs_ = Bstg[si % 4]
                        si += 1
                        p.cp("act" if ev % 2 else "dve", s_[:, 0:n], pq_[:, 0:n], [bq_], [bs_])
                        ev += 1
                        p.dma("sp", qnT_d[h, :, t0:t0 + n], s_[:, 0:n], reads=[bs_], writes=[B_q])
                        pq_ = pq[qi % 4]
                        bq_ = Bpq[qi % 4]
                        qi += 1
                        for k in range(4):
                            p.mm(pq_[0:64, 0:n], wuq[:, k, h * 192 + 128:h * 192 + 192], ln[:, k, 0:n], k == 0, k == 3,
                                 [Bw, Bln], [bq_])
                        s_ = stg[si % 4]
                        bs_ = Bstg[si % 4]
                        si += 1
                        if bi == 0:
                            p.cp("dve", s_[0:64, 0:n], pq_[0:64, 0:n], [bq_], [bs_])
                        else:
                            rope64(pq_[0:64, 0:n], [bq_], s_[0:64, 0:n], bs_, rb, n)
                        p.dma("sp", qrT_d[h, :, t0:t0 + n], s_[0:64, 0:n], reads=[bs_], writes=[B_q])
                for h in range(8):
                    pq_ = pq[qi % 4]
                    bq_ = Bpq[qi % 4]
                    qi += 1
                    for k in range(2):
                        p.mm(pq_[:, 0:n], wk[:, k, h, :], ln[:, 4 + k, 0:n], k == 0, k == 1, [Bw, Bln], [bq_])
                    s_ = stg[si % 4]
                    bs_ = Bstg[si % 4]
                    si += 1
                    p.cp("act" if ev % 2 else "dve", s_[:, 0:n], pq_[:, 0:n], [bq_], [bs_])
                    ev += 1
                    p.dma("sp", knT_d[h, :, t0:t0 + n], s_[:, 0:n], reads=[bs_], writes=[B_k])
                for tl in range(n // 128):
                    s_ = stg[si % 4]
                    bs_ = Bstg[si % 4]
                    si += 1
                    for half in range(2):
                        pq_ = pq[qi % 4]
                        bq_ = Bpq[qi % 4]
                        qi += 1
                        for k in range(2):
                            p.mm(pq_[:, :], ln[:, 4 + k, tl * 128:(tl + 1) * 128], wv[:, k, half * 4:half * 4 + 4, :].rearrange("p h d -> p (h d)"), k == 0, k == 1,
                                 [Bw, Bln], [bq_])
                        p.cp("act" if ev % 2 else "dve", s_[:, half * 512:(half + 1) * 512], pq_[:, :], [bq_], [bs_])
                        ev += 1
                    p.dma("sp", V_d[t0 + tl * 128:t0 + (tl + 1) * 128, :], s_[:, :], reads=[bs_], writes=[B_V])
                s_ = stg[si % 4]
                bs_ = Bstg[si % 4]
                si += 1
                if bi == 0:
                    p.dma("sp", krT_d[:, t0:t0 + n], kr[:, 0:n], reads=[bl], writes=[B_k])
                else:
                    rope64(kr[:, 0:n], [bl], s_[0:64, 0:n], bs_, rb, n)
                    p.dma("sp", krT_d[:, t0:t0 + n], s_[0:64, 0:n], reads=[bs_], writes=[B_k])
            p.phase_end()

    def phase_D(l):
        ctxq = l < nl - 1
        with ExitStack() as es, ExitStack() as pes:
            krT = es.enter_context(SBT("krT", [64, N], BF16))
            Bkr = Buf()
            p.dma("sp", krT[:], krT_d[:, :], reads=[B_k], writes=[Bkr])
            knT = [es.enter_context(SBT(f"knT{i}", [128, N], BF16)) for i in range(2)]
            qnT = [es.enter_context(SBT(f"qnT{i}", [128, N], BF16)) for i in range(2)]
            qrT = [es.enter_context(SBT(f"qrT{i}", [64, N], BF16)) for i in range(2)]
            gT = [es.enter_context(SBT(f"gT{i}", [128, N], BF16)) for i in range(2)]
            Vh = [es.enter_context(SBT(f"Vh{i}", [128, 18, 128], BF16)) for i in range(2)]
            Bhd = [Buf(), Buf()]
            PT = [es.enter_context(SBT(f"PT{i}", [128, 512], BF16)) for i in range(4)]
            BPT = [Buf() for _ in range(4)]
            rs = [es.enter_context(SBT(f"rs{i}", [128, 512], F32)) for i in range(2)]
            Brs = [Buf(), Buf()]
            yst = [es.enter_context(SBT(f"yst{i}", [128, 512], BF16)) for i in range(2)]
            Byst = [Buf(), Buf()]
            pS = [pes.enter_context(PST(f"pS{i}", [128, 512], F32)) for i in range(4)]
            BpS = [Buf() for _ in range(4)]
            pO = [pes.enter_context(PST(f"pO{i}", [128, 512], F32)) for i in range(2)]
            BpO = [Buf(), Buf()]
            pZ = [pes.enter_context(PST(f"pZ{i}", [128, 512], F32)) for i in range(2)]
            BpZ = [Buf(), Buf()]
            sidx = 0
            oidx = 0
            for h in range(8):
                hb = h % 2
                bh = Bhd[hb]
                p.dma("sp", knT[hb][:], knT_d[h, :, :], reads=[B_k], writes=[bh])
                p.dma("sp", qnT[hb][:], qnT_d[h, :, :], reads=[B_q], writes=[bh])
                p.dma("sp", qrT[hb][:], qrT_d[h, :, :], reads=[B_q], writes=[bh])
                p.dma("sp", gT[hb][:], uT_d[832 + h * 128:832 + (h + 1) * 128, :], reads=[B_u], writes=[bh])
                p.dma("sp", Vh[hb][:], V_d[:, h * 128:(h + 1) * 128].rearrange("(t p) d -> p t d", p=128), reads=[B_V], writes=[bh])
                for bi, (t0, n) in enumerate(TBLK):
                    if bi == 0 and not ctxq:
                        continue
                    nk = 2 if bi == 0 else 18
                    ob = oidx % 2
                    oidx += 1
                    slots = []

                    def issue_S(kc):
                        nonlocal sidx
                        s = sidx % 4
                        sidx += 1
                        p.mm(pS[s][:, 0:n], knT[hb][:, kc * 128:(kc + 1) * 128], qnT[hb][:, t0:t0 + n], True, False,
                             [bh], [BpS[s]], inc=False)
                        p.mm(pS[s][:, 0:n], krT[:, kc * 128:(kc + 1) * 128], qrT[hb][:, t0:t0 + n], False, True,
                             [bh, Bkr], [BpS[s]])
                        p.act(PT[s][:, 0:n], pS[s][:, 0:n], AF.Exp, [BpS[s]], [BPT[s]], scale=MLA_SCALE)
                        slots.append(s)
                    for kc in range(min(3, nk)):
                        issue_S(kc)
                    for kc in range(nk):
                        if kc + 3 < nk:
                            issue_S(kc + 3)
                        s = slots[kc]
                        last = kc == nk - 1
                        p.mm(pO[ob][:, 0:n], Vh[hb][:, kc, :], PT[s][:, 0:n], kc == 0, last, [bh, BPT[s]], [BpO[ob]], inc=False)
                        p.mm(pZ[ob][:, 0:n], onesB[:, :], PT[s][:, 0:n], kc == 0, last, [BPT[s], BC], [BpZ[ob], BpO[ob]], inc=True)
                    p.op("dve", lambda e, o=rs[ob][:, 0:n], i_=pZ[ob][:, 0:n]: e.reciprocal(out=o, in_=i_),
                         reads=[BpZ[ob]], writes=[Brs[ob]])
                    p.tt("dve", rs[ob][:, 0:n], pO[ob][:, 0:n], rs[ob][:, 0:n], ALU.mult, [BpO[ob], Brs[ob]], [Brs[ob]])
                    p.tt("pool", yst[ob][:, 0:n], rs[ob][:, 0:n], gT[hb][:, t0:t0 + n], ALU.mult, [Brs[ob], bh], [Byst[ob]])
                    p.dma("sp", yT_d[h * 128:(h + 1) * 128, t0:t0 + n], yst[ob][:, 0:n], reads=[Byst[ob]], writes=[B_y])
            p.phase_end()

    def phase_E(l):
        with ExitStack() as es, ExitStack() as pes:
            def sb(name, shape, dt):
                return es.enter_context(SBT(name, shape, dt))
            xb = sb("e_xb", [128, N], BF16)
            gT = sb("e_gT", [128, N], BF16)
            xf = sb("e_xf", [128, N], F32)
            xc = sb("e_xc", [128, N], F32)
            xcb = sb("e_xcb", [128, N], BF16)
            rr = sb("e_r", [128, N], F32)
            ii = sb("e_i", [128, N], F32)
            aa = sb("e_a", [128, N], F32)
            bb = sb("e_b", [128, N], F32)
            hh = [sb(f"e_h{d}", [128, N], F32) for d in range(2)]
            yo = sb("e_yo", [128, N], BF16)
            prm = sb("e_prm", [128, 16], F32)
            Wbd = sb("e_Wbd", [128, 4, 128], BF16)
            pg = [pes.enter_context(PST(f"e_pg{i}", [128, 512], F32)) for i in range(4)]
            Bpg = [Buf() for _ in range(4)]
            Bx, Bg, Bxf, Bxc, Bxcb, Br, Bi, Ba, Bb, Bprm, BW, Byo = (Buf() for _ in range(12))
            Bh = [Buf(), Buf()]
            gi = 0
            for c in range(4):
                ch = slice(c * 128, (c + 1) * 128)
                p.dma("sp", xb[:], uT_d[1856 + c * 128:1856 + (c + 1) * 128, :], reads=[B_u], writes=[Bx])
                p.dma("sp", gT[:], uT_d[2368 + c * 128:2368 + (c + 1) * 128, :], reads=[B_u], writes=[Bg])
                p.dma("sp", prm[:, 0:4], I["lru_conv_w"][l, :, ch].rearrange("k p -> p k"), writes=[Bprm], allow_slow_non_contiguous=True)
                p.dma("sp", prm[:, 4:5], I["lru_conv_b"][l, ch].rearrange("(p o) -> p o", o=1), writes=[Bprm], allow_slow_non_contiguous=True)
                for d in range(2):
                    p.dma("sp", prm[:, 5 + d:6 + d], I["lru_b_r"][l, d, ch].rearrange("(p o) -> p o", o=1), writes=[Bprm], allow_slow_non_contiguous=True)
                    p.dma("sp", prm[:, 7 + d:8 + d], I["lru_b_i"][l, d, ch].rearrange("(p o) -> p o", o=1), writes=[Bprm], allow_slow_non_contiguous=True)
                    p.dma("sp", prm[:, 9 + d:10 + d], I["lru_lambda"][l, d, ch].rearrange("(p o) -> p o", o=1), writes=[Bprm], allow_slow_non_contiguous=True)
                p.memset("pool", Wbd[:], 0.0, [BW])
                for d in range(2):
                    for gate, nm in enumerate(("lru_w_r", "lru_w_i")):
                        for blk in range(2):
                            p.dma("pool", Wbd[blk * 64:(blk + 1) * 64, d * 2 + gate, blk * 64:(blk + 1) * 64],
                                  I[nm][l, d, c * 2 + blk, :, :], writes=[BW])
                p.act(prm[:, 11:13], prm[:, 9:11], AF.Exp, [Bprm], [Bprm], scale=-1.0)
                p.act(prm[:, 11:13], prm[:, 11:13], AF.Ln, [Bprm, BC], [Bprm], bias=epsT[:, 2:3], scale=1.0)
                p.ts("dve", prm[:, 11:13], prm[:, 11:13], -8.0, None, ALU.mult, ALU.bypass, [Bprm], [Bprm])
                p.cp("dve", xf[:], xb[:], [Bx], [Bxf])
                p.act(xc[:], xf[:], AF.Identity, [Bxf, Bprm], [Bxc], bias=prm[:, 4:5], scale=prm[:, 2:3])
                for (s0, s1) in ((0, L), (L, N)):
                    for tap, sh in ((0, -2), (1, -1), (3, 1)):
                        if sh < 0:
                            o_ = xc[:, s0 - sh:s1]
                            i_ = xf[:, s0:s1 + sh]
                        else:
                            o_ = xc[:, s0:s1 - sh]
                            i_ = xf[:, s0 + sh:s1]
                        p.stt("dve", o_, i_, prm[:, tap:tap + 1], o_, ALU.mult, ALU.add, [Bxf, Bprm, Bxc], [Bxc])
                p.cp("pool", xcb[:], xc[:], [Bxc], [Bxcb])
                for d in range(2):
                    for gate, (dst, bdst) in enumerate(((rr, Br), (ii, Bi))):
                        for bi, (t0, n) in enumerate(TBLK):
                            g_ = gi % 4
                            gi += 1
                            p.mm(pg[g_][:, 0:n], Wbd[:, d * 2 + gate, :], xcb[:, t0:t0 + n], True, True, [BW, Bxcb], [Bpg[g_]])
                            p.act(dst[:, t0:t0 + n], pg[g_][:, 0:n], AF.Sigmoid, [Bpg[g_], Bprm], [bdst],
                                  bias=prm[:, 5 + 2 * gate + d:6 + 2 * gate + d], scale=1.0)
                    p.act(aa[:], rr[:], AF.Exp, [Br, Bprm], [Ba], scale=prm[:, 11 + d:12 + d])
                    p.tt("pool", bb[:], aa[:], aa[:], ALU.mult, [Ba], [Bb])
                    p.ts("dve", bb[:], bb[:], -1.0, 1.0, ALU.mult, ALU.add, [Bb], [Bb])
                    p.act(bb[:], bb[:], AF.Sqrt, [Bb], [Bb])
                    p.tt("pool", ii[:], ii[:], xc[:], ALU.mult, [Bi, Bxc], [Bi])
                    p.tt("dve", bb[:], bb[:], ii[:], ALU.mult, [Bb, Bi], [Bb])
                    h_ = hh[d]
                    if d == 0:
                        p.op("dve", lambda e, o=h_[:, 0:L], a_=aa[:, 0:L], b_=bb[:, 0:L]: e.tensor_tensor_scan(
                            out=o, data0=a_, data1=b_, initial=0.0, op0=ALU.mult, op1=ALU.add), reads=[Ba, Bb], writes=[Bh[d]])
                        p.op("dve", lambda e, o=h_[:, L:N], a_=aa[:, L:N], b_=bb[:, L:N], i_=h_[:, L - 1:L]: e.tensor_tensor_scan(
                            out=o, data0=a_, data1=b_, initial=i_, op0=ALU.mult, op1=ALU.add), reads=[Ba, Bb, Bh[d]], writes=[Bh[d]])
                    else:
                        p.op("dve", lambda e, o=h_[:, L - 1::-1], a_=aa[:, L - 1::-1], b_=bb[:, L - 1::-1]: e.tensor_tensor_scan(
                            out=o, data0=a_, data1=b_, initial=0.0, op0=ALU.mult, op1=ALU.add), reads=[Ba, Bb], writes=[Bh[d]])
                        p.op("dve", lambda e, o=h_[:, N - 1:L - 1:-1], a_=aa[:, N - 1:L - 1:-1], b_=bb[:, N - 1:L - 1:-1], i_=h_[:, 0:1]: e.tensor_tensor_scan(
                            out=o, data0=a_, data1=b_, initial=i_, op0=ALU.mult, op1=ALU.add), reads=[Ba, Bb, Bh[d]], writes=[Bh[d]])
                p.tt("pool", hh[0][:], hh[0][:], hh[1][:], ALU.add, [Bh[0], Bh[1]], [Bh[0]])
                p.tt("dve", yo[:], hh[0][:], gT[:], ALU.mult, [Bh[0], Bg], [Byo])
                p.dma("sp", yT_d[1024 + c * 128:1024 + (c + 1) * 128, :], yo[:], reads=[Byo], writes=[B_y])
            p.phase_end()

    def phase_F(l):
        with ExitStack() as es, ExitStack() as pes:
            def sb(name, shape, dt):
                return es.enter_context(SBT(name, shape, dt))

            def ps(name, shape, dt=F32):
                return pes.enter_context(PST(name, shape, dt))
            lg = sb("f_lg", [128, 8], F32)
            dec = sb("f_dec", [128, 2, RT_W], F32)
            DT = sb("f_DT", [128, 128], F32)
            qb_ = sb("f_qb", [128, N], BF16)
            kb_ = sb("f_kb", [128, N], BF16)
            qf = sb("f_qf", [128, N], F32)
            kf = sb("f_kf", [128, N], F32)
            tq = sb("f_tq", [128, 512], F32)
            qh = sb("f_qh", [128, N], BF16)
            kh = sb("f_kh", [128, N], BF16)
            qx = [sb(f"f_qx{d}", [128, N], BF16) for d in range(2)]
            vt = sb("f_vt", [128, 18, 128], BF16)
            gT = sb("f_gT", [128, N], BF16)
            rC = [sb(f"f_rC{i}", [128, 512], F32) for i in range(2)]
            rS = [sb(f"f_rS{i}", [128, 512], F32) for i in range(2)]
            kz = [sb(f"f_kz{d}", [128, 18, 128], BF16) for d in range(2)]
            sT = sb("f_sT", [128, 18, 128], BF16)
            kv = [sb(f"f_kv{d}", [128, 18, 128], F32) for d in range(2)]
            Rp = [sb(f"f_Rp{d}", [128, 18, 128], F32) for d in range(2)]
            Rb16 = [sb(f"f_Rb{d}", [128, 18, 128], BF16) for d in range(2)]
            osb = sb("f_o", [128, N], F32)
            osq = sb("f_osq", [128, 512], F32)
            mu = sb("f_mu", [128, 512], F32)
            var = sb("f_var", [128, 512], F32)
            yo = sb("f_yo", [128, N], BF16)
            pA = [ps(f"f_pA{i}", [128, 512]) for i in range(2)]
            BpA = [Buf(), Buf()]
            pB = [ps(f"f_pB{i}", [128, 128]) for i in range(4)]
            BpB = [Buf() for _ in range(4)]
            pT = [ps(f"f_pT{i}", [128, 128], BF16) for i in range(2)]
            BpT = [Buf(), Buf()]
            (Blg, Bdec, BDT, Bq, Bk, Bqf, Bkf, Btq, Bqh, Bkh, Bvt, Bg, BsT, Bo, Bosq, Bmu, Bvar, Byo) = (Buf() for _ in range(18))
            Bqx = [Buf(), Buf()]
            BrC = [Buf(), Buf()]
            Bkz = [Buf(), Buf()]
            Bkv = [Buf(), Buf()]
            BRp = [Buf(), Buf()]
            BRb = [Buf(), Buf()]
            p.dma("sp", lg[:], I["ret_decay"][l:l + 1, :].partition_broadcast(128), writes=[Blg])
            p.act(lg[:], lg[:], AF.Exp, [Blg], [Blg], scale=-1.0)
            p.act(lg[:], lg[:], AF.Ln, [Blg, BC], [Blg], bias=epsT[:, 2:3], scale=1.0)
            p.ts("dve", lg[:], lg[:], -1.0, None, ALU.mult, ALU.bypass, [Blg], [Blg])
            ai = 0
            bi_ = 0
            ti = 0
            for h in range(4):
                row = slice(h * 128, (h + 1) * 128)
                p.dma("sp", qb_[:], uT_d[2880 + h * 128:2880 + (h + 1) * 128, :], reads=[B_u], writes=[Bq])
                p.dma("sp", kb_[:], uT_d[3392 + h * 128:3392 + (h + 1) * 128, :], reads=[B_u], writes=[Bk])
                p.dma("sp", gT[:], uT_d[4416 + h * 128:4416 + (h + 1) * 128, :], reads=[B_u], writes=[Bg])
                p.dma("sp", vt[:], vtok_d[:, h * 128:(h + 1) * 128].rearrange("(t p) d -> p t d", p=128), reads=[B_v], writes=[Bvt])
                for d in range(2):
                    p.act(dec[:, d, :], rtab[:, :], AF.Exp, [BC, Blg], [Bdec], scale=lg[:, d * 4 + h:d * 4 + h + 1])
                p.tt("dve", DT[:], dec[:, 0, 0:128], dec[:, 1, 128:256], ALU.add, [Bdec], [BDT])
                p.cp("dve", qf[:], qb_[:], [Bq], [Bqf])
                p.cp("pool", kf[:], kb_[:], [Bk], [Bkf])
                p.cp("act", qh[:, 0:L], qb_[:, 0:L], [Bq], [Bqh])
                p.act(kh[:, 0:L], kb_[:, 0:L], AF.Identity, [Bk], [Bkh], scale=RET_K_SCALE)
                for bi in range(1, 5):
                    t0, n = TBLK[bi]
                    rb = bi % 2
                    p.dma("sp", rC[rb][:], I["ropeR_C"][:, t0 - L:t0 - L + n], writes=[BrC[rb]])
                    p.dma("sp", rS[rb][:], I["ropeR_S"][:, t0 - L:t0 - L + n], writes=[BrC[rb]])
                    for which, (src, bsrc, dst, bdst, scl) in enumerate(((qf, Bqf, qh, Bqh, 1.0), (kf, Bkf, kh, Bkh, RET_K_SCALE))):
                        pa = pA[ai % 2]
                        bpa = BpA[ai % 2]
                        ai += 1
                        p.mm(pa[:, :], perm128[:, :], src[:, t0:t0 + n], True, True, [bsrc, BC], [bpa])
                        p.tt("pool", tq[:], src[:, t0:t0 + n], rC[rb][:], ALU.mult, [bsrc, BrC[rb]], [Btq])
                        p.tt("dve", src[:, t0:t0 + n], pa[:, :], rS[rb][:], ALU.mult, [bpa, BrC[rb]], [bsrc])
                        if scl == 1.0:
                            p.tt("dve", dst[:, t0:t0 + n], src[:, t0:t0 + n], tq[:], ALU.add, [bsrc, Btq], [bdst])
                        else:
                            p.tt("dve", tq[:], src[:, t0:t0 + n], tq[:], ALU.add, [bsrc, Btq], [Btq])
                            p.act(dst[:, t0:t0 + n], tq[:], AF.Identity, [Btq], [bdst], scale=scl)
                for d in range(2):
                    xi = dec[:, d, 256 + d * 128:384 + d * 128]
                    p.tt("pool" if d else "dve", qx[d][:].rearrange("p (c j) -> p c j", j=128),
                         qh[:].rearrange("p (c j) -> p c j", j=128),
                         xi.unsqueeze(1).to_broadcast([128, 18, 128]), ALU.mult, [Bqh, Bdec], [Bqx[d]])
                for c in range(18):
                    cs = slice(c * 128, (c + 1) * 128)
                    b_ = bi_ % 4
                    bi_ += 1
                    p.mm(pB[b_][:, :], kh[:, cs], qh[:, cs], True, True, [Bkh, Bqh], [BpB[b_]])
                    p.tt("dve", sT[:, c, :], pB[b_][:, :], DT[:], ALU.mult, [BpB[b_], BDT], [BsT])
                    t_ = ti % 2
                    ti += 1
                    p.tr(pT[t_][:, :], kh[:, cs], identB[:], [Bkh, BC], [BpT[t_]])
                    p.act(kz[0][:, c, :], pT[t_][:, :], AF.Identity, [BpT[t_], Bdec], [Bkz[0]], scale=dec[:, 0, 512:513])
                    p.ts("dve", kz[1][:, c, :], pT[t_][:, :], dec[:, 1, 513:514], None, ALU.mult, ALU.bypass, [BpT[t_], Bdec], [Bkz[1]])
                    for d in range(2):
                        b_ = bi_ % 4
                        bi_ += 1
                        p.mm(pB[b_][:, :], kz[d][:, c, :], vt[:, c, :], True, True, [Bkz[d], Bvt], [BpB[b_]])
                        p.cp("act" if d else "dve", kv[d][:, c, :], pB[b_][:, :], [BpB[b_]], [Bkv[d]])
                g128 = [dec[:, d, RT_W - 1:RT_W] for d in range(2)]
                p.memset("pool", Rp[0][:, 0, :], 0.0, [BRp[0]])
                p.cp("pool", Rp[0][:, 1, :], kv[0][:, 0, :], [Bkv[0]], [BRp[0]])
                p.memset("pool", Rp[1][:, 1, :], 0.0, [BRp[1]])
                p.cp("pool", Rp[1][:, 0, :], kv[1][:, 1, :], [Bkv[1]], [BRp[1]])
                p.stt("dve", Rp[0][:, 2, :], kv[0][:, 0, :], g128[0], kv[0][:, 1, :], ALU.mult, ALU.add, [Bkv[0], Bdec], [BRp[0]])
                for c in range(3, 18):
                    p.stt("dve", Rp[0][:, c, :], Rp[0][:, c - 1, :], g128[0], kv[0][:, c - 1, :], ALU.mult, ALU.add,
                          [Bkv[0], Bdec, BRp[0]], [BRp[0]])
                p.stt("dve", Rp[1][:, 17, :], kv[1][:, 1, :], g128[1], kv[1][:, 0, :], ALU.mult, ALU.add, [Bkv[1], Bdec], [BRp[1]])
                for c in range(16, 1, -1):
                    p.stt("dve", Rp[1][:, c, :], Rp[1][:, c + 1, :], g128[1], kv[1][:, c + 1, :], ALU.mult, ALU.add,
                          [Bkv[1], Bdec, BRp[1]], [BRp[1]])
                p.cp("act", Rb16[0][:], Rp[0][:], [BRp[0]], [BRb[0]])
                p.cp("pool", Rb16[1][:], Rp[1][:], [BRp[1]], [BRb[1]])
                for blk, (t0, n) in enumerate(TBLK):
                    pa = pA[ai % 2]
                    bpa = BpA[ai % 2]
                    ai += 1
                    for j in range(n // 128):
                        c = t0 // 128 + j
                        cs = slice(c * 128, (c + 1) * 128)
                        o_ = pa[:, j * 128:(j + 1) * 128]
                        p.mm(o_, vt[:, c, :], sT[:, c, :], True, False, [Bvt, BsT], [bpa], inc=False)
                        p.mm(o_, Rb16[0][:, c, :], qx[0][:, cs], False, False, [BRb[0], Bqx[0]], [bpa], inc=False)
                        p.mm(o_, Rb16[1][:, c, :], qx[1][:, cs], False, True, [BRb[1], Bqx[1]], [bpa], inc=True)
                    p.cp("act", osb[:, t0:t0 + n], pa[:, 0:n], [bpa], [Bo])
                    p.tt("pool", osq[:, 0:n], osb[:, t0:t0 + n], osb[:, t0:t0 + n], ALU.mult, [Bo], [Bosq])
                    pm_ = pA[ai % 2]
                    bpm_ = BpA[ai % 2]
                    ai += 1
                    p.mm(pm_[:, 0:n], onesF[:, :], osb[:, t0:t0 + n], True, True, [Bo, BC], [bpm_])
                    p.cp("act", mu[:, 0:n], pm_[:, 0:n], [bpm_], [Bmu])
                    pv_ = pA[ai % 2]
                    bpv_ = BpA[ai % 2]
                    ai += 1
                    p.mm(pv_[:, 0:n], onesF[:, :], osq[:, 0:n], True, True, [Bosq, BC], [bpv_])
                    p.tt("pool", osq[:, 0:n], mu[:, 0:n], mu[:, 0:n], ALU.mult, [Bmu], [Bosq])
                    p.tt("dve", var[:, 0:n], pv_[:, 0:n], osq[:, 0:n], ALU.subtract, [bpv_, Bosq], [Bvar])
                    p.act(var[:, 0:n], var[:, 0:n], AF.Sqrt, [Bvar, BC], [Bvar], bias=epsT[:, 0:1], scale=1.0)
                    p.op("dve", lambda e, o=var[:, 0:n]: e.reciprocal(out=o, in_=o), reads=[Bvar], writes=[Bvar])
                    p.tt("pool", mu[:, 0:n], osb[:, t0:t0 + n], mu[:, 0:n], ALU.subtract, [Bo, Bmu], [Bmu])
                    p.tt("dve", mu[:, 0:n], mu[:, 0:n], var[:, 0:n], ALU.mult, [Bmu, Bvar], [Bmu])
                    p.tt("dve", yo[:, t0:t0 + n], mu[:, 0:n], gT[:, t0:t0 + n], ALU.mult, [Bmu, Bg], [Byo])
                p.dma("sp", yT_d[1536 + h * 128:1536 + (h + 1) * 128, :], yo[:], reads=[Byo], writes=[B_y])
            p.phase_end()

    def phase_G(l):
        last = l == nl - 1
        with ExitStack() as es, ExitStack() as pes:
            wo = es.enter_context(SBT("g_wo", [128, 16, D], BF16))
            Bw = Buf()
            for q4 in range(4):
                p.dma("pool", wo[:, :, q4 * 512:(q4 + 1) * 512],
                      I["w_out"][l, :, q4 * 512:(q4 + 1) * 512].rearrange("(k p) c -> p k c", p=128), writes=[Bw])
            yb = [es.enter_context(SBT(f"g_y{i}", [128, 16, 128], BF16)) for i in range(2)]
            Byb = [Buf(), Buf()]
            ht = [es.enter_context(SBT(f"g_h{i}", [128, D], F32)) for i in range(2)]
            Bht = [Buf(), Buf()]
            tmp = [es.enter_context(SBT(f"g_t{i}", [128, D], F32)) for i in range(2)]
            Btmp = [Buf(), Buf()]
            st = es.enter_context(SBT("g_st", [128, 2, 4, 6], F32))
            mv = es.enter_context(SBT("g_mv", [128, 2, 4], F32))
            Bst = [Buf(), Buf()]
            pz = [pes.enter_context(PST(f"g_pz{i}", [128, 512], F32)) for i in range(8)]
            Bpz = [Buf() for _ in range(8)]
            tiles = list(range(2, 18)) if last else list(range(18))
            for it, t in enumerate(tiles):
                b2 = it % 2
                r = 1 if t < 2 else 0
                tok = slice(t * 128, (t + 1) * 128)
                p.dma("sp", yb[b2][:], yT_d[:, tok].rearrange("(k p) t -> p k t", p=128), reads=[B_y], writes=[Byb[b2]])
                p.dma("sp", ht[b2][:], h_d[tok, :], reads=[B_h], writes=[Bht[b2]])
                for q4 in range(4):
                    pz_ = pz[b2 * 4 + q4]
                    bz_ = Bpz[b2 * 4 + q4]
                    for k in range(16):
                        p.mm(pz_[:, :], yb[b2][:, k, :], wo[:, k, q4 * 512:(q4 + 1) * 512], k == 0, k == 15, [Byb[b2], Bw], [bz_])
                    cs = slice(q4 * 512, (q4 + 1) * 512)
                    p.tt("dve", tmp[b2][:, cs], pz_[:, :], gt_bc[:, r, cs], ALU.mult, [bz_, Bmod], [Btmp[b2]])
                    p.stt("dve", tmp[b2][:, cs], ht[b2][:, cs], ALPHA, tmp[b2][:, cs], ALU.mult, ALU.add, [Bht[b2], Btmp[b2]], [Btmp[b2]])
                    p.op("dve", lambda e, o=st[:, b2, q4, :], i_=tmp[b2][:, cs]: e.bn_stats(out=o, in_=i_),
                         reads=[Btmp[b2]], writes=[Bst[b2]])
                p.op("dve", lambda e, o=mv[:, b2, 0:2], i_=st[:, b2, :, :].rearrange("p c s -> p (c s)"): e.bn_aggr(out=o, in_=i_), reads=[Bst[b2]], writes=[Bst[b2]])
                p.act(mv[:, b2, 2:3], mv[:, b2, 1:2], AF.Sqrt, [Bst[b2], BC], [Bst[b2]], bias=epsT[:, 0:1], scale=1.0)
                p.op("dve", lambda e, o=mv[:, b2, 2:3]: e.reciprocal(out=o, in_=o), reads=[Bst[b2]], writes=[Bst[b2]])
                p.stt("dve", mv[:, b2, 3:4], mv[:, b2, 0:1], -1.0, mv[:, b2, 2:3], ALU.mult, ALU.mult, [Bst[b2]], [Bst[b2]])
                p.act(tmp[b2][:], tmp[b2][:], AF.Identity, [Bst[b2], Btmp[b2]], [Btmp[b2]], bias=mv[:, b2, 3:4], scale=mv[:, b2, 2:3])
                p.tt("pool", tmp[b2][:], tmp[b2][:], lng_bc[:], ALU.mult, [Btmp[b2], Bmod], [Btmp[b2]])
                p.tt("dve", ht[b2][:], tmp[b2][:], lnb_bc[:], ALU.add, [Btmp[b2], Bmod, Bht[b2]], [Bht[b2]])
                if last:
                    p.dma("sp", out_d[(t - 2) * 128:(t - 1) * 128, :], ht[b2][:], reads=[Bht[b2]])
                else:
                    p.dma("sp", h_d[tok, :], ht[b2][:], reads=[Bht[b2]], writes=[B_h])
            p.phase_end()

    phases = [("ada", phase_ada), ("AB", phase_AB), ("C", phase_C), ("D", phase_D), ("E", phase_E), ("F", phase_F), ("G", phase_G)]
    done = False
    for l in range(nl):
        for nm, fn in phases:
            fn(l)
            if stop_after == (l, nm):
                done = True
                break
        if done:
            break
    waits = []
    for s, c in p.dma_cnt.items():
        if p.known["sp"].get(s, 0) < 16 * c:
            waits.append((s, 16 * c))
    if waits:
        p.ops["sp"].append((waits, None, None))
        p.flush()
    p.stack.close()
    return nc


_CACHE = {}


def make_in_maps(inputs, consts):
    f = lambda a: np.ascontiguousarray(np.asarray(a, dtype=np.float32))
    shared = {}
    for n_, shp in W_NAMES:
        shared[n_] = f(inputs[n_]).reshape(shp)
    for n_, shp in C_NAMES:
        shared[n_] = f(consts[n_]).reshape(shp)
    x = f(inputs["x"])
    ctx = f(inputs["ctx"])
    c = f(inputs["c"])
    cc = f(inputs["c_ctx"])
    maps = []
    for core in range(8):
        b = core % 4
        m = dict(shared)
        m["x"] = x[b]
        m["ctx"] = ctx[b]
        m["c2"] = np.ascontiguousarray(np.stack([c[b], cc], axis=0))
        maps.append(m)
    return maps


def kernel(**inputs):
    if "nc" not in _CACHE:
        _CACHE["nc"] = build()
    nc = _CACHE["nc"]
    maps = make_in_maps(inputs, host_consts())
    res = run_bass_kernel_spmd(nc, maps, core_ids=list(range(8)))
    out = np.stack([np.asarray(res.results[b]["out"], dtype=np.float32) for b in range(4)], axis=0)
    return out
```

```python
import numpy as np
from contextlib import ExitStack
import concourse.bass as bass
import concourse.mybir as mybir
from concourse.bass_utils import run_bass_kernel_spmd

F32 = mybir.dt.float32
BF16 = mybir.dt.bfloat16
AF = mybir.ActivationFunctionType
ALU = mybir.AluOpType

D = 2048
T = 2048
L = 256
N = T + L
NL = 4
MIX_IN = 4928
ALPHA = float(8 ** 0.25)
LN_EPS = 1e-5
RMS_EPS = 1e-6
MLA_SCALE = float(192 ** -0.5)
RET_K_SCALE = float(128 ** -0.5)
BIG = 1.0e6
TBLK = [(0, 256), (256, 512), (768, 512), (1280, 512), (1792, 512)]


class Buf:
    __slots__ = ("name", "w", "r")

    def __init__(self, name="b"):
        self.name = name
        self.w = None
        self.r = {}


class Stream:
    def __init__(self, sid, engs):
        self.sid = sid
        self.cnt = {e: 0 for e in engs}
        self.known = {e: {} for e in engs}
        self.dma_cnt = {}
        self.dma_rr = {e: 0 for e in engs}
        self.ops = []


class Prog:
    ENG = ("pe", "act", "dve", "pool", "sp")
    NSTREAM = 3

    def __init__(self, nc, ndma=8):
        self.nc = nc
        self.ops = {e: [] for e in self.ENG}
        self.streams = [Stream(i, self.ENG) for i in range(self.NSTREAM)]
        self.cur = self.streams[0]
        self.ndma = ndma
        self.stack = ExitStack()
        self.nalloc = 0
        self.sems = {}
        self.NDMA = {0: {"sp": 8, "pool": 6}, 1: {"sp": 6, "pool": 0}, 2: {"sp": 4, "pool": 2}}
        for sid in range(self.NSTREAM):
            for e in ("pe", "act", "dve", "pool"):
                self.sems[f"E{sid}:{e}"] = self.stack.enter_context(nc.semaphore(f"E{sid}_{e}"))
            for q in ("sp", "pool"):
                for j in range(self.NDMA[sid][q]):
                    self.sems[f"D{sid}:{q}:{j}"] = self.stack.enter_context(nc.semaphore(f"D{sid}_{q}_{j}"))

    def stream(self, sid):
        self.cur = self.streams[sid]

    def sbuf(self, shape, dtype, name=None):
        self.nalloc += 1
        return self.stack.enter_context(self.nc.sbuf_tensor(name or f"sb{self.nalloc}", list(shape), dtype))

    def _deps(self, eng, reads, writes):
        st = self.cur
        w = {}

        def add(ev):
            if ev is None:
                return
            s, v = ev
            if w.get(s, 0) < v:
                w[s] = v
        for b in reads:
            add(b.w)
        for b in writes:
            add(b.w)
            for s, v in b.r.items():
                add((s, v))
        out = []
        for s, v in w.items():
            if eng == "pe" and s[0] == "E" and s.endswith(":pe"):
                continue
            if st.known[eng].get(s, 0) >= v:
                continue
            st.known[eng][s] = v
            out.append((s, v))
        return out

    def _mark(self, ev, reads, writes):
        for b in writes:
            b.w = ev
            b.r = {}
        for b in reads:
            if b.r.get(ev[0], 0) < ev[1]:
                b.r[ev[0]] = ev[1]

    def op(self, eng, fn, reads=(), writes=(), inc=True):
        st = self.cur
        waits = self._deps(eng, reads, writes)
        s = f"E{st.sid}:{eng}"
        ev = (s, st.cnt[eng] + 1)
        if inc:
            st.cnt[eng] += 1
        st.ops.append((eng, waits, fn, (s, 1) if inc else None))
        self._mark(ev, reads, writes)
        return ev

    def dma(self, q, out_ap, in_ap, reads=(), writes=(), **kw):
        st = self.cur
        waits = self._deps(q, reads, writes)
        j = st.dma_rr[q]
        st.dma_rr[q] = (j + 1) % self.NDMA[st.sid][q]
        s = f"D{st.sid}:{q}:{j}"
        c = st.dma_cnt.get(s, 0)
        if c > 0 and st.known[q].get(s, 0) < 16 * c:
            waits.append((s, 16 * c))
            st.known[q][s] = 16 * c
        st.dma_cnt[s] = c + 1
        ev = (s, 16 * (c + 1))
        st.ops.append((q, waits, (lambda e, o=out_ap, i=in_ap, k=kw: e.dma_start(out=o, in_=i, **k)), (s, 16)))
        self._mark(ev, reads, writes)
        return ev

    def merge(self):
        allops = []
        for st in self.streams:
            n = len(st.ops)
            for i, o in enumerate(st.ops):
                allops.append(((i + 0.5) / n, st.sid, i, o))
            st.ops = []
        allops.sort(key=lambda t: (t[0], t[1], t[2]))
        for _, _, _, (eng, waits, fn, inc) in allops:
            self.ops[eng].append((waits, fn, inc))

    def barrier(self):
        self.merge()
        tot = {}
        for st in self.streams:
            for e in self.ENG:
                if st.cnt[e]:
                    tot[f"E{st.sid}:{e}"] = st.cnt[e]
            for s, c in st.dma_cnt.items():
                tot[s] = 16 * c
        main = self.streams[0]
        for e in self.ENG:
            waits = []
            for s, v in tot.items():
                if main.known[e].get(s, 0) >= v:
                    continue
                waits.append((s, v))
            for st in self.streams:
                for s, v in tot.items():
                    st.known[e][s] = v
            if waits:
                self.ops[e].append((waits, None, None))

    def final_dma_wait(self):
        self.merge()
        waits = []
        for st in self.streams:
            for s, c in st.dma_cnt.items():
                if self.streams[0].known["sp"].get(s, 0) < 16 * c:
                    waits.append((s, 16 * c))
        if waits:
            self.ops["sp"].append((waits, None, None))

    def flush(self):
        self.merge()
        nc = self.nc
        sems = self.sems
        with nc.Block() as block:
            def run(eo, lst):
                for waits, fn, inc in lst:
                    for s_, v in waits:
                        eo.wait_ge(sems[s_], v)
                    if fn is not None:
                        ins = fn(eo)
                        if inc:
                            ins.then_inc(sems[inc[0]], inc[1])

            if self.ops["pe"]:
                @block.tensor
                def _(e):
                    run(e, self.ops["pe"])
            if self.ops["act"]:
                @block.scalar
                def _(e):
                    run(e, self.ops["act"])
            if self.ops["dve"]:
                @block.vector
                def _(e):
                    run(e, self.ops["dve"])
            if self.ops["pool"]:
                @block.gpsimd
                def _(e):
                    run(e, self.ops["pool"])
            if self.ops["sp"]:
                @block.sync
                def _(e):
                    run(e, self.ops["sp"])
        self.ops = {e: [] for e in self.ENG}

    def phase_end(self):
        self.barrier()
        self.flush()

    def mm(self, out, lhsT, rhs, start, stop, reads, writes, inc=None):
        if inc is None:
            inc = stop
        return self.op("pe", lambda e: e.matmul(out, lhsT=lhsT, rhs=rhs, start=start, stop=stop),
                       reads=reads, writes=writes, inc=inc)

    def tr(self, out, in_, ident, reads, writes, inc=True):
        return self.op("pe", lambda e: e.transpose(out=out, in_=in_, identity=ident), reads=reads, writes=writes, inc=inc)

    def act(self, out, in_, func, reads, writes, bias=None, scale=None):
        kw = {}
        if bias is not None:
            kw["bias"] = bias
        if scale is not None:
            kw["scale"] = scale
        return self.op("act", lambda e: e.activation(out=out, in_=in_, func=func, **kw), reads=reads, writes=writes)

    def cp(self, eng, out, in_, reads, writes):
        if eng == "act":
            return self.op("act", lambda e: e.copy(out=out, in_=in_), reads=reads, writes=writes)
        return self.op(eng, lambda e: e.tensor_copy(out=out, in_=in_), reads=reads, writes=writes)

    def tt(self, eng, out, a, b, op, reads, writes):
        return self.op(eng, lambda e: e.tensor_tensor(out=out, in0=a, in1=b, op=op), reads=reads, writes=writes)

    def ts(self, eng, out, a, s1, s2, op0, op1, reads, writes):
        if s2 is None:
            return self.op(eng, lambda e: e.tensor_single_scalar(out=out, in_=a, scalar=s1, op=op0), reads=reads, writes=writes)
        return self.op(eng, lambda e: e.tensor_scalar(out=out, in0=a, scalar1=s1, scalar2=s2, op0=op0, op1=op1),
                       reads=reads, writes=writes)

    def stt(self, eng, out, a, s, b, op0, op1, reads, writes):
        return self.op(eng, lambda e: e.scalar_tensor_tensor(out=out, in0=a, scalar=s, in1=b, op0=op0, op1=op1),
                       reads=reads, writes=writes)

    def memset(self, eng, ap, val, writes):
        return self.op(eng, lambda e: e.memset(ap, val), writes=writes)


def host_consts():
    c = {}
    c["ident"] = np.eye(128, dtype=np.float32)

    def perm(dim):
        q = dim // 4
        P = np.zeros((dim, dim), np.float32)
        for a in range(2):
            for s in range(2):
                for f in range(q):
                    i = a * 2 * q + s * q + f
                    j = a * 2 * q + (1 - s) * q + f
                    P[j, i] = 1.0
        return P
    c["perm64"] = perm(64)
    c["perm128"] = perm(128)
    sel = np.zeros((2, 2, 128), np.float32)
    sel[0, 0, :] = 1.0
    sel[1, 1, :] = 1.0
    c["sel"] = sel.reshape(2, 256)

    def rope(dim, scale):
        rows = T // 64
        row = np.repeat(np.arange(rows, dtype=np.float32), 64)
        col = np.tile(np.arange(64, dtype=np.float32), rows)
        q = dim // 4
        inv = (np.float32(10000.0) ** (-np.arange(q, dtype=np.float32) / np.float32(q))).astype(np.float32)
        ang = np.stack([row[:, None] * inv, col[:, None] * inv], axis=1).astype(np.float32)
        cos = np.cos(ang).astype(np.float32)
        sin = np.sin(ang).astype(np.float32)
        C = np.zeros((dim, T), np.float32)
        S = np.zeros((dim, T), np.float32)
        for a in range(2):
            for s in range(2):
                i0 = a * 2 * q + s * q
                C[i0:i0 + q, :] = cos[:, a, :].T
                S[i0:i0 + q, :] = (sin[:, a, :].T) * (-1.0 if s == 0 else 1.0)
        return (C * np.float32(scale)).astype(np.float32), (S * np.float32(scale)).astype(np.float32)
    c["ropeM_C"], c["ropeM_S"] = rope(64, 1.0)
    c["ropeR_C"], c["ropeR_S"] = rope(128, 1.0)
    k = np.arange(128, dtype=np.float32)[:, None]
    q = np.arange(128, dtype=np.float32)[None, :]
    EDf = np.where(q >= k, q - k, BIG)
    EDb = np.where(k > q, k - q, BIG)
    EXIf = np.broadcast_to(q + 1.0, (128, 128))
    EXIb = np.broadcast_to(128.0 - q, (128, 128))
    EZf = 127.0 - k
    EZb = k
    EWf = np.concatenate([255.0 - k, 255.0 - (128.0 + k)], axis=1)
    EWb = np.concatenate([k, 128.0 + k], axis=1)
    c128 = np.full((128, 1), 128.0, np.float32)
    c["rtab"] = np.concatenate([EDf, EDb, EXIf, EXIb, EZf, EZb, EWf, EWb, c128], axis=1).astype(np.float32)
    return c


RT_W = 128 * 4 + 7

W_NAMES = [("w_ada", [NL, D, 6144]), ("b_ada", [NL, 6144]), ("w_in", [NL, D, MIX_IN]),
           ("mla_q_norm_g", [NL, 512]), ("mla_kv_norm_g", [NL, 256]),
           ("mla_w_uq", [NL, 512, 1536]), ("mla_w_ukv", [NL, 256, 2048]),
           ("lru_conv_w", [NL, 4, 512]), ("lru_conv_b", [NL, 512]),
           ("lru_w_r", [NL, 2, 8, 64, 64]), ("lru_b_r", [NL, 2, 512]),
           ("lru_w_i", [NL, 2, 8, 64, 64]), ("lru_b_i", [NL, 2, 512]),
           ("lru_lambda", [NL, 2, 512]), ("ret_decay", [NL, 8]),
           ("w_out", [NL, D, D]), ("ln_g", [NL, D]), ("ln_b", [NL, D])]
C_NAMES = [("ident", [128, 128]), ("perm64", [64, 64]), ("perm128", [128, 128]), ("sel", [2, 256]),
           ("ropeM_C", [64, T]), ("ropeM_S", [64, T]), ("ropeR_C", [128, T]), ("ropeR_S", [128, T]),
           ("rtab", [128, RT_W])]


def build(nl=NL, dbg=(), stop_after=None):
    nc = bass.Bass("TRN2", target_bir_lowering=False)
    p = Prog(nc)
    uid = [0]

    def SBT(name, shape, dt):
        uid[0] += 1
        return nc.sbuf_tensor(f"{name}_{uid[0]}", list(shape), dt)

    def PST(name, shape, dt):
        uid[0] += 1
        return nc.psum_tensor(f"{name}_{uid[0]}", list(shape), dt)
    I = {}
    I["x"] = nc.dram_tensor("x", [T, D], F32, kind="ExternalInput").ap()
    I["ctx"] = nc.dram_tensor("ctx", [L, D], F32, kind="ExternalInput").ap()
    I["c2"] = nc.dram_tensor("c2", [2, D], F32, kind="ExternalInput").ap()
    for n_, shp in W_NAMES + C_NAMES:
        I[n_] = nc.dram_tensor(n_, shp, F32, kind="ExternalInput").ap()
    out_d = nc.dram_tensor("out", [T, D], F32, kind="ExternalOutput").ap()

    def scr(name, shape, dt):
        kind = "ExternalOutput" if name in dbg else "Internal"
        return nc.dram_tensor(name, list(shape), dt, kind=kind).ap()
    h_d = scr("h_d", [N, D], F32)
    uT_d = scr("uT_d", [MIX_IN, N], BF16)
    vtok_d = scr("vtok_d", [N, 512], BF16)
    qnT_d = scr("qnT_d", [8, 128, N], BF16)
    qrT_d = scr("qrT_d", [8, 64, N], BF16)
    knT_d = scr("knT_d", [8, 128, N], BF16)
    krT_d = scr("krT_d", [64, N], BF16)
    V_d = scr("V_d", [N, 1024], BF16)
    yT_d = scr("yT_d", [D, N], BF16)
    B_h, B_u, B_v, B_q, B_k, B_V, B_yD, B_yE, B_yF = (Buf() for _ in range(9))

    identF = p.sbuf([128, 128], F32)
    identB = p.sbuf([128, 128], BF16)
    onesB = p.sbuf([128, 128], BF16)
    onesF = p.sbuf([128, 128], F32)
    perm64 = p.sbuf([64, 64], F32)
    perm128 = p.sbuf([128, 128], F32)
    selT = p.sbuf([2, 256], F32)
    epsT = p.sbuf([128, 4], F32)
    cT = p.sbuf([128, 16, 2], BF16)
    modT = p.sbuf([128, 32, 2], F32)
    gt_bc = p.sbuf([128, 2, D], F32)
    lng_bc = p.sbuf([128, D], F32)
    lnb_bc = p.sbuf([128, D], F32)
    rtab = p.sbuf([128, RT_W], F32)
    BC = Buf("const")
    Bmod = Buf("mod")

    QE = "sp"

    class OwnCtx:
        def __enter__(self):
            self.es = ExitStack()
            self.pes = ExitStack()
            return self.es, self.pes

        def __exit__(self, *a):
            if a[0] is None:
                p.phase_end()
            self.pes.close()
            self.es.close()
            return False

    class SharedCtx:
        def __init__(self, es, pes):
            self.es, self.pes = es, pes

        def __enter__(self):
            return self.es, self.pes

        def __exit__(self, *a):
            return False

    def run_concurrent(l, plist):
        with ExitStack() as es, ExitStack() as pes:
            for sid, fn in plist:
                p.stream(sid)
                fn(l, SharedCtx(es, pes))
            p.stream(0)
            p.phase_end()

    with ExitStack() as es:
        cF = es.enter_context(SBT("cF", [128, 16, 2], F32))
        p.dma("sp", identF[:], I["ident"][:, :], writes=[BC])
        p.dma("sp", perm64[:], I["perm64"][:, :], writes=[BC])
        p.dma("sp", perm128[:], I["perm128"][:, :], writes=[BC])
        p.dma("sp", selT[:], I["sel"][:, :], writes=[BC])
        p.dma("sp", rtab[:], I["rtab"][:, :], writes=[BC])
        Bc = Buf()
        for r in range(2):
            p.dma("sp", cF[:, :, r], I["c2"][r, :].rearrange("(k p) -> p k", p=128), writes=[Bc], allow_slow_non_contiguous=True)
        p.memset("dve", onesB[:], 1.0, [BC])
        p.memset("dve", onesF[:], 1.0 / 128.0, [BC])
        p.memset("dve", epsT[:, 0:1], LN_EPS, [BC])
        p.memset("dve", epsT[:, 1:2], RMS_EPS, [BC])
        p.memset("dve", epsT[:, 2:3], 1.0, [BC])
        p.memset("dve", epsT[:, 3:4], 0.0, [BC])
        p.cp("dve", identB[:], identF[:], [BC], [BC])
        p.act(cT[:], cF[:], AF.Silu, [Bc], [BC])
        p.phase_end()

    def phase_ada(l, ctx=None):
        with (ctx or OwnCtx()) as (es, pes):
            wt = [es.enter_context(SBT(f"adaw{i}", [128, 16, 512], BF16)) for i in range(2)]
            Bw = [Buf(), Buf()]
            badaT = es.enter_context(SBT("badaT", [128, 32], F32))
            gtrow = es.enter_context(SBT("gtrow", [2, D], F32))
            brow = es.enter_context(SBT("brow", [2, D], F32))
            ps_f = pes.enter_context(PST("ada_f", [128, 32, 2], F32))
            ps_r = [pes.enter_context(PST(f"ada_r{i}", [2, 512], F32)) for i in range(2)]
            ps_b = [pes.enter_context(PST(f"ada_b{i}", [128, 512], F32)) for i in range(2)]
            Bpf, Bpr, Bpb = Buf(), [Buf(), Buf()], [Buf(), Buf()]
            Bb, Bg = Buf(), Buf()
            p.dma("sp", badaT[:], I["b_ada"][l, 0:4096].rearrange("(j p) -> p j", p=128), writes=[Bb],
                  allow_slow_non_contiguous=True)
            p.dma("sp", brow[:], I["b_ada"][l:l + 1, 4096:6144].partition_broadcast(2), writes=[Bb])
            p.dma("sp", lng_bc[:], I["ln_g"][l:l + 1, :].partition_broadcast(128), writes=[Bmod])
            p.dma("sp", lnb_bc[:], I["ln_b"][l:l + 1, :].partition_broadcast(128), writes=[Bmod])
            for g in range(12):
                w = wt[g % 2]
                bw = Bw[g % 2]
                p.dma("pool", w[:], I["w_ada"][l, :, g * 512:(g + 1) * 512].rearrange("(k p) c -> p k c", p=128),
                      writes=[bw])
                if g < 8:
                    for cc in range(4):
                        j = g * 4 + cc
                        for k in range(16):
                            p.mm(ps_f[:, j, :], w[:, k, cc * 128:(cc + 1) * 128], cT[:, k, :], k == 0, k == 15,
                                 [bw, BC], [Bpf])
                else:
                    gg = g - 8
                    pr = ps_r[gg % 2]
                    for k in range(16):
                        p.mm(pr[:, :], cT[:, k, :], w[:, k, :], k == 0, k == 15, [bw, BC], [Bpr[gg % 2]])
                    p.tt("dve", gtrow[:, gg * 512:(gg + 1) * 512], pr[:, :], brow[:, gg * 512:(gg + 1) * 512], ALU.add,
                         [Bpr[gg % 2], Bb], [Bg])
            for r in range(2):
                p.tt("dve", modT[:, :, r], ps_f[:, :, r], badaT[:, :], ALU.add, [Bpf, Bb], [Bmod])
            p.ts("dve", modT[:, 16:32, :], modT[:, 16:32, :], 1.0, None, ALU.add, ALU.bypass, [Bmod], [Bmod])
            i = 0
            for r in range(2):
                for gg in range(4):
                    pb = ps_b[i % 2]
                    p.mm(pb[:, :], selT[:, r * 128:(r + 1) * 128], gtrow[:, gg * 512:(gg + 1) * 512], True, True,
                         [Bg, BC], [Bpb[i % 2]])
                    p.cp("act", gt_bc[:, r, gg * 512:(gg + 1) * 512], pb[:, :], [Bpb[i % 2]], [Bmod])
                    i += 1

    def phase_AB(l, ctx=None):
        with (ctx or OwnCtx()) as (es, pes):
            xmT = es.enter_context(SBT("xmT", [128, 16, N], BF16))
            Bx = [Buf() for _ in range(9)]
            hb = [es.enter_context(SBT(f"hblk{i}", [128, 2, D], F32)) for i in range(2)]
            Bhb = [Buf(), Buf()]
            st = es.enter_context(SBT("lnst", [128, 2, 4, 6], F32))
            mv = es.enter_context(SBT("lnmv", [128, 2, 4], F32))
            Bst = Buf()
            pT = [pes.enter_context(PST(f"pT{i}", [128, 256], F32)) for i in range(3)]
            BpT = [Buf() for _ in range(3)]
            ti = 0
            for u in range(9):
                h = hb[u % 2]
                bh = Bhb[u % 2]
                r = 1 if u == 0 else 0
                t0 = u * 256
                if l == 0:
                    src = I["ctx"] if u == 0 else I["x"][(u - 1) * 256:u * 256, :]
                    p.dma("sp", h[:], src.rearrange("(a p) d -> p a d", p=128), writes=[bh])
                    for a in range(2):
                        for c4 in range(4):
                            p.op("dve", lambda e, o=st[:, a, c4, :], i_=h[:, a, c4 * 512:(c4 + 1) * 512]: e.bn_stats(out=o, in_=i_),
                                 reads=[bh], writes=[Bst])
                        p.op("dve", lambda e, o=mv[:, a, 0:2], i_=st[:, a, :, :].rearrange("p c s -> p (c s)"): e.bn_aggr(out=o, in_=i_),
                             reads=[Bst], writes=[Bst])
                        p.act(mv[:, a, 2:3], mv[:, a, 1:2], AF.Sqrt, [Bst, BC], [Bst], bias=epsT[:, 0:1], scale=1.0)
                        p.op("dve", lambda e, o=mv[:, a, 2:3]: e.reciprocal(out=o, in_=o), reads=[Bst], writes=[Bst])
                        p.stt("dve", mv[:, a, 3:4], mv[:, a, 0:1], -1.0, mv[:, a, 2:3], ALU.mult, ALU.mult, [Bst], [Bst])
                        p.act(h[:, a, :], h[:, a, :], AF.Identity, [Bst, bh], [bh], bias=mv[:, a, 3:4], scale=mv[:, a, 2:3])
                    p.dma("sp", h_d[t0:t0 + 256, :].rearrange("(a p) d -> p a d", p=128), h[:], reads=[bh], writes=[B_h])
                else:
                    p.dma("sp", h[:], h_d[t0:t0 + 256, :].rearrange("(a p) d -> p a d", p=128), reads=[B_h], writes=[bh])
                for j in range(16):
                    pt = pT[ti % 3]
                    bp = BpT[ti % 3]
                    ti += 1
                    for a in range(2):
                        p.tr(pt[:, a * 128:(a + 1) * 128], h[:, a, j * 128:(j + 1) * 128], identF[:], [bh, BC], [bp],
                             inc=(a == 1))
                    p.act(xmT[:, j, t0:t0 + 256], pt[:, :], AF.Identity, [bp, Bmod], [Bx[u]],
                          bias=modT[:, j, r:r + 1], scale=modT[:, 16 + j, r:r + 1])
            wt = [es.enter_context(SBT(f"winw{i}", [128, 16, 512], BF16)) for i in range(2)]
            Bw = [Buf(), Buf()]
            ost = [es.enter_context(SBT(f"ost{i}", [128, N], BF16)) for i in range(3)]
            Bo = [Buf() for _ in range(3)]
            pm = [pes.enter_context(PST(f"pm{i}", [128, 512], F32)) for i in range(4)]
            Bpm = [Buf() for _ in range(4)]
            groups = [(0, 512), (512, 320)] + [(832 + 512 * i, 512) for i in range(8)]
            pi = 0
            oi = 0
            ev = 0
            for gi, (c0, wd) in enumerate(groups):
                w = wt[gi % 2]
                bw = Bw[gi % 2]
                p.dma("pool", w[:, :, 0:wd], I["w_in"][l, :, c0:c0 + wd].rearrange("(k p) c -> p k c", p=128), writes=[bw])
                if c0 == 3904:
                    for tile in range(18):
                        pmm = pm[pi % 4]
                        bpm = Bpm[pi % 4]
                        pi += 1
                        for k in range(16):
                            p.mm(pmm[:, :], xmT[:, k, tile * 128:(tile + 1) * 128], w[:, k, :], k == 0, k == 15,
                                 [bw, Bx[tile // 2]], [bpm])
                        o = ost[oi % 3]
                        bo = Bo[oi % 3]
                        oi += 1
                        p.cp("act" if ev % 2 else "dve", o[:, 0:512], pmm[:, :], [bpm], [bo])
                        ev += 1
                        p.dma("sp", vtok_d[tile * 128:(tile + 1) * 128, :], o[:, 0:512], reads=[bo], writes=[B_v])
                    continue
                nch = (wd + 127) // 128
                for cc in range(nch):
                    cw = min(128, wd - cc * 128)
                    col = c0 + cc * 128
                    is_gate = (832 <= col < 1856) or (2368 <= col < 2880) or (4416 <= col)
                    o = ost[oi % 3]
                    bo = Bo[oi % 3]
                    oi += 1
                    for bi, (t0, n) in enumerate(TBLK):
                        pmm = pm[pi % 4]
                        bpm = Bpm[pi % 4]
                        pi += 1
                        rd = [bw] + ([Bx[0]] if bi == 0 else [Bx[2 * bi - 1], Bx[2 * bi]])
                        for k in range(16):
                            p.mm(pmm[0:cw, 0:n], w[:, k, cc * 128:cc * 128 + cw], xmT[:, k, t0:t0 + n], k == 0, k == 15,
                                 rd, [bpm])
                        if is_gate:
                            p.act(o[0:cw, t0:t0 + n], pmm[0:cw, 0:n], AF.Silu, [bpm], [bo])
                        else:
                            p.cp("dve", o[0:cw, t0:t0 + n], pmm[0:cw, 0:n], [bpm], [bo])
                    p.dma("sp", uT_d[col:col + cw, :], o[0:cw, :], reads=[bo], writes=[B_u])

    def phase_C(l, ctx=None):
        ctxq = l < nl - 1
        with (ctx or OwnCtx()) as (es, pes):
            wuq = es.enter_context(SBT("wuq", [128, 4, 1536], BF16))
            wk = es.enter_context(SBT("wk", [128, 2, 8, 128], BF16))
            wv = es.enter_context(SBT("wv", [128, 2, 8, 128], BF16))
            gq = es.enter_context(SBT("gq", [128, 6], F32))
            Bw = Buf()
            p.dma("pool", wuq[:], I["mla_w_uq"][l].rearrange("(k p) c -> p k c", p=128), writes=[Bw])
            ukv = I["mla_w_ukv"][l].rearrange("(k p) (h s d) -> p k h s d", p=128, h=8, s=2)
            for k in range(2):
                p.dma("pool", wk[:, k, :, :], ukv[:, k, :, 0, :], writes=[Bw])
                p.dma("pool", wv[:, k, :, :], ukv[:, k, :, 1, :], writes=[Bw])
            p.dma("sp", gq[:, 0:4], I["mla_q_norm_g"][l].rearrange("(k p) -> p k", p=128), writes=[Bw],
                  allow_slow_non_contiguous=True)
            p.dma("sp", gq[:, 4:6], I["mla_kv_norm_g"][l].rearrange("(k p) -> p k", p=128), writes=[Bw],
                  allow_slow_non_contiguous=True)
            lat = [es.enter_context(SBT(f"lat{i}", [128, 6, 512], BF16)) for i in range(2)]
            krb = [es.enter_context(SBT(f"krb{i}", [64, 512], BF16)) for i in range(2)]
            Blat = [Buf(), Buf()]
            sq = es.enter_context(SBT("sq", [128, 6, 512], BF16))
            Bsq = Buf()
            rstd = es.enter_context(SBT("rstd", [128, 2, 512], F32))
            Brs = Buf()
            ln = es.enter_context(SBT("lnq", [128, 6, 512], BF16))
            Bln = Buf()
            ropeC = [es.enter_context(SBT(f"rpc{i}", [64, 512], F32)) for i in range(2)]
            ropeS = [es.enter_context(SBT(f"rps{i}", [64, 512], F32)) for i in range(2)]
            Brope = [Buf(), Buf()]
            xr = [es.enter_context(SBT(f"xr{i}", [64, 512], F32)) for i in range(2)]
            t1 = [es.enter_context(SBT(f"t1{i}", [64, 512], F32)) for i in range(2)]
            Bxr = [Buf(), Buf()]
            Bt1 = [Buf(), Buf()]
            stg = [es.enter_context(SBT(f"stg{i}", [128, 1024], BF16)) for i in range(4)]
            Bstg = [Buf() for _ in range(4)]
            pss = [pes.enter_context(PST(f"pss{i}", [128, 512], F32)) for i in range(2)]
            Bpss = [Buf(), Buf()]
            pq = [pes.enter_context(PST(f"pq{i}", [128, 512], F32)) for i in range(4)]
            Bpq = [Buf() for _ in range(4)]
            pp = [pes.enter_context(PST(f"pp{i}", [64, 512], F32)) for i in range(2)]
            Bpp = [Buf(), Buf()]
            qi = 0
            si = 0
            ri = 0
            ev = 0

            def rope64(src_ps, src_reads, dst, dst_buf, rb, n):
                nonlocal ri
                x_ = xr[ri % 2]
                bx_ = Bxr[ri % 2]
                t_ = t1[ri % 2]
                bt_ = Bt1[ri % 2]
                pp_ = pp[ri % 2]
                bpp_ = Bpp[ri % 2]
                ri += 1
                p.cp("act", x_[:, 0:n], src_ps, src_reads, [bx_])
                p.mm(pp_[:, 0:n], perm64[:, :], x_[:, 0:n], True, True, [bx_, BC], [bpp_])
                p.tt("pool", t_[:, 0:n], x_[:, 0:n], ropeC[rb][:, 0:n], ALU.mult, [bx_, Brope[rb]], [bt_])
                p.tt("dve", x_[:, 0:n], pp_[:, 0:n], ropeS[rb][:, 0:n], ALU.mult, [bpp_, Brope[rb]], [bx_])
                p.tt("dve", dst, x_[:, 0:n], t_[:, 0:n], ALU.add, [bx_, bt_], [dst_buf])

            def c_load(bi_):
                t0_, n_ = TBLK[bi_]
                p.dma("sp", lat[bi_ % 2][:, :, 0:n_], uT_d[0:768, t0_:t0_ + n_].rearrange("(k p) t -> p k t", p=128), reads=[B_u], writes=[Blat[bi_ % 2]])
                p.dma("sp", krb[bi_ % 2][:, 0:n_], uT_d[768:832, t0_:t0_ + n_], reads=[B_u], writes=[Blat[bi_ % 2]])
                if bi_ > 0:
                    p.dma("sp", ropeC[bi_ % 2][:, 0:n_], I["ropeM_C"][:, t0_ - L:t0_ - L + n_], writes=[Brope[bi_ % 2]])
                    p.dma("sp", ropeS[bi_ % 2][:, 0:n_], I["ropeM_S"][:, t0_ - L:t0_ - L + n_], writes=[Brope[bi_ % 2]])
            c_load(0)
            for bi, (t0, n) in enumerate(TBLK):
                la = lat[bi % 2]
                kr = krb[bi % 2]
                bl = Blat[bi % 2]
                rb = bi % 2
                if bi + 1 < len(TBLK):
                    c_load(bi + 1)
                p.tt("pool", sq[:, :, 0:n], la[:, :, 0:n], la[:, :, 0:n], ALU.mult, [bl], [Bsq])
                for which, (k0, k1, dim) in enumerate(((0, 4, 512.0), (4, 6, 256.0))):
                    ps_ = pss[which]
                    for k in range(k0, k1):
                        p.mm(ps_[:, 0:n], onesB[:, :], sq[:, k, 0:n], k == k0, k == k1 - 1, [Bsq, BC], [Bpss[which]])
                    p.act(rstd[:, which, 0:n], ps_[:, 0:n], AF.Sqrt, [Bpss[which], BC], [Brs], bias=epsT[:, 1:2], scale=1.0 / dim)
                    p.op("dve", lambda e, o=rstd[:, which, 0:n]: e.reciprocal(out=o, in_=o), reads=[Brs], writes=[Brs])
                    for k in range(k0, k1):
                        p.stt("dve", ln[:, k, 0:n], la[:, k, 0:n], gq[:, k:k + 1], rstd[:, which, 0:n], ALU.mult, ALU.mult,
                              [bl, Bw, Brs], [Bln])
                if bi > 0 or ctxq:
                    for h in range(8):
                        pq_ = pq[qi % 4]
                        bq_ = Bpq[qi % 4]
                        qi += 1
                        for k in range(4):
                            p.mm(pq_[:, 0:n], wuq[:, k, h * 192:h * 192 + 128], ln[:, k, 0:n], k == 0, k == 3, [Bw, Bln], [bq_])
                        s_ = stg[si % 4]
                        bs_ = Bstg[si % 4]
                        si += 1
                        p.cp("act" if ev % 2 else "dve", s_[:, 0:n], pq_[:, 0:n], [bq_], [bs_])
                        ev += 1
                        p.dma("sp", qnT_d[h, :, t0:t0 + n], s_[:, 0:n], reads=[bs_], writes=[B_q])
                        pq_ = pq[qi % 4]
                        bq_ = Bpq[qi % 4]
                        qi += 1
                        for k in range(4):
                            p.mm(pq_[0:64, 0:n], wuq[:, k, h * 192 + 128:h * 192 + 192], ln[:, k, 0:n], k == 0, k == 3,
                                 [Bw, Bln], [bq_])
                        s_ = stg[si % 4]
                        bs_ = Bstg[si % 4]
                        si += 1
                        if bi == 0:
                            p.cp("dve", s_[0:64, 0:n], pq_[0:64, 0:n], [bq_], [bs_])
                        else:
                            rope64(pq_[0:64, 0:n], [bq_], s_[0:64, 0:n], bs_, rb, n)
                        p.dma("sp", qrT_d[h, :, t0:t0 + n], s_[0:64, 0:n], reads=[bs_], writes=[B_q])
                for h in range(8):
                    pq_ = pq[qi % 4]
                    bq_ = Bpq[qi % 4]
                    qi += 1
                    for k in range(2):
                        p.mm(pq_[:, 0:n], wk[:, k, h, :], ln[:, 4 + k, 0:n], k == 0, k == 1, [Bw, Bln], [bq_])
                    s_ = stg[si % 4]
                    bs_ = Bstg[si % 4]
                    si += 1
                    p.cp("act" if ev % 2 else "dve", s_[:, 0:n], pq_[:, 0:n], [bq_], [bs_])
                    ev += 1
                    p.dma("sp", knT_d[h, :, t0:t0 + n], s_[:, 0:n], reads=[bs_], writes=[B_k])
                for tl in range(n // 128):
                    s_ = stg[si % 4]
                    bs_ = Bstg[si % 4]
                    si += 1
                    for half in range(2):
                        pq_ = pq[qi % 4]
                        bq_ = Bpq[qi % 4]
                        qi += 1
                        for k in range(2):
                            p.mm(pq_[:, :], ln[:, 4 + k, tl * 128:(tl + 1) * 128], wv[:, k, half * 4:half * 4 + 4, :].rearrange("p h d -> p (h d)"), k == 0, k == 1,
                                 [Bw, Bln], [bq_])
                        p.cp("act" if ev % 2 else "dve", s_[:, half * 512:(half + 1) * 512], pq_[:, :], [bq_], [bs_])
                        ev += 1
                    p.dma("sp", V_d[t0 + tl * 128:t0 + (tl + 1) * 128, :], s_[:, :], reads=[bs_], writes=[B_V])
                s_ = stg[si % 4]
                bs_ = Bstg[si % 4]
                si += 1
                if bi == 0:
                    p.dma("sp", krT_d[:, t0:t0 + n], kr[:, 0:n], reads=[bl], writes=[B_k])
                else:
                    rope64(kr[:, 0:n], [bl], s_[0:64, 0:n], bs_, rb, n)
                    p.dma("sp", krT_d[:, t0:t0 + n], s_[0:64, 0:n], reads=[bs_], writes=[B_k])

    def phase_D(l, ctx=None):
        ctxq = l < nl - 1
        with (ctx or OwnCtx()) as (es, pes):
            krT = es.enter_context(SBT("krT", [64, N], BF16))
            Bkr = Buf()
            p.dma("sp", krT[:], krT_d[:, :], reads=[B_k], writes=[Bkr])
            knT = [es.enter_context(SBT(f"knT{i}", [128, N], BF16)) for i in range(2)]
            qnT = [es.enter_context(SBT(f"qnT{i}", [128, N], BF16)) for i in range(2)]
            qrT = [es.enter_context(SBT(f"qrT{i}", [64, N], BF16)) for i in range(2)]
            gT = [es.enter_context(SBT(f"gT{i}", [128, N], BF16)) for i in range(2)]
            Vh = [es.enter_context(SBT(f"Vh{i}", [128, 18, 128], BF16)) for i in range(2)]
            Bhd = [Buf(), Buf()]
            NS = 3
            PT = [es.enter_context(SBT(f"PT{i}", [128, 512], BF16)) for i in range(NS)]
            BPT = [Buf() for _ in range(NS)]
            rs = [es.enter_context(SBT(f"rs{i}", [128, 512], F32)) for i in range(2)]
            Brs = [Buf(), Buf()]
            yst = [es.enter_context(SBT(f"yst{i}", [128, 512], BF16)) for i in range(2)]
            Byst = [Buf(), Buf()]
            pS = [pes.enter_context(PST(f"pS{i}", [128, 512], F32)) for i in range(NS)]
            BpS = [Buf() for _ in range(NS)]
            pO = [pes.enter_context(PST(f"pO{i}", [128, 512], F32)) for i in range(2)]
            BpO = [Buf(), Buf()]
            pZ = [pes.enter_context(PST(f"pZ{i}", [128, 512], F32)) for i in range(2)]
            BpZ = [Buf(), Buf()]
            sidx = 0
            oidx = 0

            def load_head(h):
                hb = h % 2
                bh = Bhd[hb]
                p.dma("sp", knT[hb][:], knT_d[h, :, :], reads=[B_k], writes=[bh])
                p.dma("sp", qnT[hb][:], qnT_d[h, :, :], reads=[B_q], writes=[bh])
                p.dma("sp", qrT[hb][:], qrT_d[h, :, :], reads=[B_q], writes=[bh])
                p.dma("sp", gT[hb][:], uT_d[832 + h * 128:832 + (h + 1) * 128, :], reads=[B_u], writes=[bh])
                p.dma("sp", Vh[hb][:], V_d[:, h * 128:(h + 1) * 128].rearrange("(t p) d -> p t d", p=128), reads=[B_V], writes=[bh])
            load_head(0)
            for h in range(8):
                hb = h % 2
                bh = Bhd[hb]
                if h + 1 < 8:
                    load_head(h + 1)
                for bi, (t0, n) in enumerate(TBLK):
                    if bi == 0 and not ctxq:
                        continue
                    nk = 2 if bi == 0 else 18
                    ob = oidx % 2
                    oidx += 1
                    slots = []

                    def issue_S(kc):
                        nonlocal sidx
                        s = sidx % NS
                        sidx += 1
                        p.mm(pS[s][:, 0:n], knT[hb][:, kc * 128:(kc + 1) * 128], qnT[hb][:, t0:t0 + n], True, False,
                             [bh], [BpS[s]], inc=False)
                        p.mm(pS[s][:, 0:n], krT[:, kc * 128:(kc + 1) * 128], qrT[hb][:, t0:t0 + n], False, True,
                             [bh, Bkr], [BpS[s]])
                        p.act(PT[s][:, 0:n], pS[s][:, 0:n], AF.Exp, [BpS[s]], [BPT[s]], scale=MLA_SCALE)
                        slots.append(s)
                    for kc in range(min(NS - 1, nk)):
                        issue_S(kc)
                    for kc in range(nk):
                        if kc + NS - 1 < nk:
                            issue_S(kc + NS - 1)
                        s = slots[kc]
                        last = kc == nk - 1
                        p.mm(pO[ob][:, 0:n], Vh[hb][:, kc, :], PT[s][:, 0:n], kc == 0, last, [bh, BPT[s]], [BpO[ob]], inc=False)
                        p.mm(pZ[ob][:, 0:n], onesB[:, :], PT[s][:, 0:n], kc == 0, last, [BPT[s], BC], [BpZ[ob], BpO[ob]], inc=True)
                    p.op("dve", lambda e, o=rs[ob][:, 0:n], i_=pZ[ob][:, 0:n]: e.reciprocal(out=o, in_=i_),
                         reads=[BpZ[ob]], writes=[Brs[ob]])
                    p.tt("dve", rs[ob][:, 0:n], pO[ob][:, 0:n], rs[ob][:, 0:n], ALU.mult, [BpO[ob], Brs[ob]], [Brs[ob]])
                    p.tt("pool", yst[ob][:, 0:n], rs[ob][:, 0:n], gT[hb][:, t0:t0 + n], ALU.mult, [Brs[ob], bh], [Byst[ob]])
                    p.dma("sp", yT_d[h * 128:(h + 1) * 128, t0:t0 + n], yst[ob][:, 0:n], reads=[Byst[ob]], writes=[B_yD])

    def phase_E(l, ctx=None):
        with (ctx or OwnCtx()) as (es, pes):
            def sb(name, shape, dt):
                return es.enter_context(SBT(name, shape, dt))
            xb = sb("e_xb", [128, N], BF16)
            gT = sb("e_gT", [128, N], BF16)
            xf = sb("e_xf", [128, N], F32)
            xc = sb("e_xc", [128, N], F32)
            xcb = sb("e_xcb", [128, N], BF16)
            rr = sb("e_r", [128, N], F32)
            ii = sb("e_i", [128, N], F32)
            aa = sb("e_a", [128, N], F32)
            bb = sb("e_b", [128, N], F32)
            hh = [sb(f"e_h{d}", [128, N], F32) for d in range(2)]
            yo = sb("e_yo", [128, N], BF16)
            prm = sb("e_prm", [128, 16], F32)
            Wbd = sb("e_Wbd", [128, 4, 128], BF16)
            pg = [pes.enter_context(PST("e_pg", [128, 512], F32))]
            Bpg = [Buf()]
            Bx, Bg, Bxf, Bxc, Bxcb, Br, Bi, Ba, Bb, Bprm, BW, Byo = (Buf() for _ in range(12))
            Bh = [Buf(), Buf()]
            gi = 0
            for c in range(4):
                ch = slice(c * 128, (c + 1) * 128)
                p.dma(QE, xb[:], uT_d[1856 + c * 128:1856 + (c + 1) * 128, :], reads=[B_u], writes=[Bx])
                p.dma(QE, gT[:], uT_d[2368 + c * 128:2368 + (c + 1) * 128, :], reads=[B_u], writes=[Bg])
                p.dma(QE, prm[:, 0:4], I["lru_conv_w"][l, :, ch].rearrange("k p -> p k"), writes=[Bprm], allow_slow_non_contiguous=True)
                p.dma(QE, prm[:, 4:5], I["lru_conv_b"][l, ch].rearrange("(p o) -> p o", o=1), writes=[Bprm], allow_slow_non_contiguous=True)
                for d in range(2):
                    p.dma(QE, prm[:, 5 + d:6 + d], I["lru_b_r"][l, d, ch].rearrange("(p o) -> p o", o=1), writes=[Bprm], allow_slow_non_contiguous=True)
                    p.dma(QE, prm[:, 7 + d:8 + d], I["lru_b_i"][l, d, ch].rearrange("(p o) -> p o", o=1), writes=[Bprm], allow_slow_non_contiguous=True)
                    p.dma(QE, prm[:, 9 + d:10 + d], I["lru_lambda"][l, d, ch].rearrange("(p o) -> p o", o=1), writes=[Bprm], allow_slow_non_contiguous=True)
                p.memset("pool", Wbd[:], 0.0, [BW])
                for d in range(2):
                    for gate, nm in enumerate(("lru_w_r", "lru_w_i")):
                        for blk in range(2):
                            p.dma("pool", Wbd[blk * 64:(blk + 1) * 64, d * 2 + gate, blk * 64:(blk + 1) * 64],
                                  I[nm][l, d, c * 2 + blk, :, :], writes=[BW])
                p.act(prm[:, 11:13], prm[:, 9:11], AF.Exp, [Bprm], [Bprm], scale=-1.0)
                p.act(prm[:, 11:13], prm[:, 11:13], AF.Ln, [Bprm, BC], [Bprm], bias=epsT[:, 2:3], scale=1.0)
                p.ts("dve", prm[:, 11:13], prm[:, 11:13], -4.0, None, ALU.mult, ALU.bypass, [Bprm], [Bprm])
                p.ts("dve", prm[:, 5:9], prm[:, 5:9], 0.5, None, ALU.mult, ALU.bypass, [Bprm], [Bprm])
                p.cp("dve", xf[:], xb[:], [Bx], [Bxf])
                p.act(xc[:], xf[:], AF.Identity, [Bxf, Bprm], [Bxc], bias=prm[:, 4:5], scale=prm[:, 2:3])
                for (s0, s1) in ((0, L), (L, N)):
                    for tap, sh in ((0, -2), (1, -1), (3, 1)):
                        if sh < 0:
                            o_ = xc[:, s0 - sh:s1]
                            i_ = xf[:, s0:s1 + sh]
                        else:
                            o_ = xc[:, s0:s1 - sh]
                            i_ = xf[:, s0 + sh:s1]
                        p.stt("dve", o_, i_, prm[:, tap:tap + 1], o_, ALU.mult, ALU.add, [Bxf, Bprm, Bxc], [Bxc])
                p.cp("pool", xcb[:], xc[:], [Bxc], [Bxcb])
                for d in range(2):
                    for gate, (dst, bdst) in enumerate(((rr, Br), (ii, Bi))):
                        for bi, (t0, n) in enumerate(TBLK):
                            g_ = 0
                            p.mm(pg[g_][:, 0:n], Wbd[:, d * 2 + gate, :], xcb[:, t0:t0 + n], True, True, [BW, Bxcb], [Bpg[g_]])
                            p.act(dst[:, t0:t0 + n], pg[g_][:, 0:n], AF.Tanh, [Bpg[g_], Bprm], [bdst],
                                  bias=prm[:, 5 + 2 * gate + d:6 + 2 * gate + d], scale=0.5)
                    p.act(aa[:], rr[:], AF.Exp, [Br, Bprm], [Ba], bias=prm[:, 11 + d:12 + d], scale=prm[:, 11 + d:12 + d])
                    p.tt("pool", bb[:], aa[:], aa[:], ALU.mult, [Ba], [Bb])
                    p.ts("dve", bb[:], bb[:], -1.0, 1.0, ALU.mult, ALU.add, [Bb], [Bb])
                    p.act(bb[:], bb[:], AF.Sqrt, [Bb], [Bb], scale=0.25)
                    p.stt("dve", ii[:], ii[:], 1.0, xc[:], ALU.add, ALU.mult, [Bi, Bxc], [Bi])
                    p.tt("dve", bb[:], bb[:], ii[:], ALU.mult, [Bb, Bi], [Bb])
                    h_ = hh[d]
                    if d == 0:
                        p.op("dve", lambda e, o=h_[:, 0:L], a_=aa[:, 0:L], b_=bb[:, 0:L]: e.tensor_tensor_scan(
                            out=o, data0=a_, data1=b_, initial=0.0, op0=ALU.mult, op1=ALU.add), reads=[Ba, Bb], writes=[Bh[d]])
                        p.op("dve", lambda e, o=h_[:, L:N], a_=aa[:, L:N], b_=bb[:, L:N], i_=h_[:, L - 1:L]: e.tensor_tensor_scan(
                            out=o, data0=a_, data1=b_, initial=i_, op0=ALU.mult, op1=ALU.add), reads=[Ba, Bb, Bh[d]], writes=[Bh[d]])
                    else:
                        p.op("dve", lambda e, o=h_[:, L - 1::-1], a_=aa[:, L - 1::-1], b_=bb[:, L - 1::-1]: e.tensor_tensor_scan(
                            out=o, data0=a_, data1=b_, initial=0.0, op0=ALU.mult, op1=ALU.add), reads=[Ba, Bb], writes=[Bh[d]])
                        p.op("dve", lambda e, o=h_[:, N - 1:L - 1:-1], a_=aa[:, N - 1:L - 1:-1], b_=bb[:, N - 1:L - 1:-1], i_=h_[:, 0:1]: e.tensor_tensor_scan(
                            out=o, data0=a_, data1=b_, initial=i_, op0=ALU.mult, op1=ALU.add), reads=[Ba, Bb, Bh[d]], writes=[Bh[d]])
                p.tt("pool", hh[0][:], hh[0][:], hh[1][:], ALU.add, [Bh[0], Bh[1]], [Bh[0]])
                p.tt("dve", yo[:], hh[0][:], gT[:], ALU.mult, [Bh[0], Bg], [Byo])
                p.dma(QE, yT_d[1024 + c * 128:1024 + (c + 1) * 128, :], yo[:], reads=[Byo], writes=[B_yE])

    def phase_F(l, ctx=None):
        with (ctx or OwnCtx()) as (es, pes):
            def sb(name, shape, dt):
                return es.enter_context(SBT(name, shape, dt))

            def ps(name, shape, dt=F32):
                return pes.enter_context(PST(name, shape, dt))
            lg = sb("f_lg", [128, 8], F32)
            dec = sb("f_dec", [128, 2, RT_W], F32)
            DT = sb("f_DT", [128, 128], F32)
            qb_ = sb("f_qb", [128, N], BF16)
            kb_ = sb("f_kb", [128, N], BF16)
            qf = sb("f_qf", [128, N], F32)
            kf = sb("f_kf", [128, N], F32)
            tq = sb("f_tq", [128, 512], F32)
            qh = sb("f_qh", [128, N], BF16)
            kh = sb("f_kh", [128, N], BF16)
            qx = [sb(f"f_qx{d}", [128, N], BF16) for d in range(2)]
            vt = sb("f_vt", [128, 18, 128], BF16)
            gT = sb("f_gT", [128, N], BF16)
            rC = [sb(f"f_rC{i}", [128, 512], F32) for i in range(2)]
            rS = [sb(f"f_rS{i}", [128, 512], F32) for i in range(2)]
            kz = [sb(f"f_kz{d}", [128, 18, 128], BF16) for d in range(2)]
            sT = sb("f_sT", [128, 18, 128], BF16)
            kv = [sb(f"f_kv{d}", [128, 18, 128], F32) for d in range(2)]
            Rp = [sb(f"f_Rp{d}", [128, 18, 128], F32) for d in range(2)]
            Rb16 = [sb(f"f_Rb{d}", [128, 18, 128], BF16) for d in range(2)]
            osb = sb("f_o", [128, N], F32)
            osq = sb("f_osq", [128, 512], F32)
            mu = sb("f_mu", [128, 512], F32)
            var = sb("f_var", [128, 512], F32)
            yo = sb("f_yo", [128, N], BF16)
            pA = [ps(f"f_pA{i}", [128, 512]) for i in range(2)]
            BpA = [Buf(), Buf()]
            pB = [ps(f"f_pB{i}", [128, 128]) for i in range(4)]
            BpB = [Buf() for _ in range(4)]
            pT = [ps(f"f_pT{i}", [128, 128], BF16) for i in range(2)]
            BpT = [Buf(), Buf()]
            (Blg, Bdec, BDT, Bq, Bk, Bqf, Bkf, Btq, Bqh, Bkh, Bvt, Bg, BsT, Bo, Bosq, Bmu, Bvar, Byo) = (Buf() for _ in range(18))
            Bqx = [Buf(), Buf()]
            BrC = [Buf(), Buf()]
            Bkz = [Buf(), Buf()]
            Bkv = [Buf(), Buf()]
            BRp = [Buf(), Buf()]
            BRb = [Buf(), Buf()]
            p.dma("sp", lg[:], I["ret_decay"][l:l + 1, :].partition_broadcast(128), writes=[Blg])
            p.act(lg[:], lg[:], AF.Exp, [Blg], [Blg], scale=-1.0)
            p.act(lg[:], lg[:], AF.Ln, [Blg, BC], [Blg], bias=epsT[:, 2:3], scale=1.0)
            p.ts("dve", lg[:], lg[:], -1.0, None, ALU.mult, ALU.bypass, [Blg], [Blg])
            ai = 0
            bi_ = 0
            ti = 0
            for h in range(4):
                row = slice(h * 128, (h + 1) * 128)
                p.dma("sp", qb_[:], uT_d[2880 + h * 128:2880 + (h + 1) * 128, :], reads=[B_u], writes=[Bq])
                p.dma("sp", kb_[:], uT_d[3392 + h * 128:3392 + (h + 1) * 128, :], reads=[B_u], writes=[Bk])
                p.dma("sp", gT[:], uT_d[4416 + h * 128:4416 + (h + 1) * 128, :], reads=[B_u], writes=[Bg])
                p.dma("sp", vt[:], vtok_d[:, h * 128:(h + 1) * 128].rearrange("(t p) d -> p t d", p=128), reads=[B_v], writes=[Bvt])
                for d in range(2):
                    p.act(dec[:, d, :], rtab[:, :], AF.Exp, [BC, Blg], [Bdec], scale=lg[:, d * 4 + h:d * 4 + h + 1])
                p.tt("dve", DT[:], dec[:, 0, 0:128], dec[:, 1, 128:256], ALU.add, [Bdec], [BDT])
                p.cp("dve", qf[:], qb_[:], [Bq], [Bqf])
                p.cp("act", kf[:], kb_[:], [Bk], [Bkf])
                p.cp("act", qh[:, 0:L], qb_[:, 0:L], [Bq], [Bqh])
                p.act(kh[:, 0:L], kb_[:, 0:L], AF.Identity, [Bk], [Bkh], scale=RET_K_SCALE)
                for bi in range(1, 5):
                    t0, n = TBLK[bi]
                    rb = bi % 2
                    p.dma("sp", rC[rb][:], I["ropeR_C"][:, t0 - L:t0 - L + n], writes=[BrC[rb]])
                    p.dma("sp", rS[rb][:], I["ropeR_S"][:, t0 - L:t0 - L + n], writes=[BrC[rb]])
                    for which, (src, bsrc, dst, bdst, scl) in enumerate(((qf, Bqf, qh, Bqh, 1.0), (kf, Bkf, kh, Bkh, RET_K_SCALE))):
                        pa = pA[ai % 2]
                        bpa = BpA[ai % 2]
                        ai += 1
                        p.mm(pa[:, :], perm128[:, :], src[:, t0:t0 + n], True, True, [bsrc, BC], [bpa])
                        p.tt("pool", tq[:], src[:, t0:t0 + n], rC[rb][:], ALU.mult, [bsrc, BrC[rb]], [Btq])
                        p.tt("dve", src[:, t0:t0 + n], pa[:, :], rS[rb][:], ALU.mult, [bpa, BrC[rb]], [bsrc])
                        if scl == 1.0:
                            p.tt("dve", dst[:, t0:t0 + n], src[:, t0:t0 + n], tq[:], ALU.add, [bsrc, Btq], [bdst])
                        else:
                            p.tt("dve", tq[:], src[:, t0:t0 + n], tq[:], ALU.add, [bsrc, Btq], [Btq])
                            p.act(dst[:, t0:t0 + n], tq[:], AF.Identity, [Btq], [bdst], scale=scl)
                for d in range(2):
                    xi = dec[:, d, 256 + d * 128:384 + d * 128]
                    p.tt("pool" if d else "dve", qx[d][:].rearrange("p (c j) -> p c j", j=128),
                         qh[:].rearrange("p (c j) -> p c j", j=128),
                         xi.unsqueeze(1).to_broadcast([128, 18, 128]), ALU.mult, [Bqh, Bdec], [Bqx[d]])
                for c in range(18):
                    cs = slice(c * 128, (c + 1) * 128)
                    b_ = bi_ % 4
                    bi_ += 1
                    p.mm(pB[b_][:, :], kh[:, cs], qh[:, cs], True, True, [Bkh, Bqh], [BpB[b_]])
                    p.tt("dve", sT[:, c, :], pB[b_][:, :], DT[:], ALU.mult, [BpB[b_], BDT], [BsT])
                    t_ = ti % 2
                    ti += 1
                    p.tr(pT[t_][:, :], kh[:, cs], identB[:], [Bkh, BC], [BpT[t_]])
                    p.act(kz[0][:, c, :], pT[t_][:, :], AF.Identity, [BpT[t_], Bdec], [Bkz[0]], scale=dec[:, 0, 512:513])
                    p.ts("dve", kz[1][:, c, :], pT[t_][:, :], dec[:, 1, 513:514], None, ALU.mult, ALU.bypass, [BpT[t_], Bdec], [Bkz[1]])
                    for d in range(2):
                        b_ = bi_ % 4
                        bi_ += 1
                        p.mm(pB[b_][:, :], kz[d][:, c, :], vt[:, c, :], True, True, [Bkz[d], Bvt], [BpB[b_]])
                        p.cp("act" if d else "dve", kv[d][:, c, :], pB[b_][:, :], [BpB[b_]], [Bkv[d]])
                g128 = [dec[:, d, RT_W - 1:RT_W] for d in range(2)]
                p.memset("pool", Rp[0][:, 0, :], 0.0, [BRp[0]])
                p.cp("pool", Rp[0][:, 1, :], kv[0][:, 0, :], [Bkv[0]], [BRp[0]])
                p.memset("pool", Rp[1][:, 1, :], 0.0, [BRp[1]])
                p.cp("pool", Rp[1][:, 0, :], kv[1][:, 1, :], [Bkv[1]], [BRp[1]])
                p.stt("dve", Rp[0][:, 2, :], kv[0][:, 0, :], g128[0], kv[0][:, 1, :], ALU.mult, ALU.add, [Bkv[0], Bdec], [BRp[0]])
                for c in range(3, 18):
                    p.stt("dve", Rp[0][:, c, :], Rp[0][:, c - 1, :], g128[0], kv[0][:, c - 1, :], ALU.mult, ALU.add,
                          [Bkv[0], Bdec, BRp[0]], [BRp[0]])
                p.stt("dve", Rp[1][:, 17, :], kv[1][:, 1, :], g128[1], kv[1][:, 0, :], ALU.mult, ALU.add, [Bkv[1], Bdec], [BRp[1]])
                for c in range(16, 1, -1):
                    p.stt("dve", Rp[1][:, c, :], Rp[1][:, c + 1, :], g128[1], kv[1][:, c + 1, :], ALU.mult, ALU.add,
                          [Bkv[1], Bdec, BRp[1]], [BRp[1]])
                p.cp("act", Rb16[0][:], Rp[0][:], [BRp[0]], [BRb[0]])
                p.cp("dve", Rb16[1][:], Rp[1][:], [BRp[1]], [BRb[1]])
                for blk, (t0, n) in enumerate(TBLK):
                    pa = pA[ai % 2]
                    bpa = BpA[ai % 2]
                    ai += 1
                    for j in range(n // 128):
                        c = t0 // 128 + j
                        cs = slice(c * 128, (c + 1) * 128)
                        o_ = pa[:, j * 128:(j + 1) * 128]
                        p.mm(o_, vt[:, c, :], sT[:, c, :], True, False, [Bvt, BsT], [bpa], inc=False)
                        p.mm(o_, Rb16[0][:, c, :], qx[0][:, cs], False, False, [BRb[0], Bqx[0]], [bpa], inc=False)
                        p.mm(o_, Rb16[1][:, c, :], qx[1][:, cs], False, True, [BRb[1], Bqx[1]], [bpa], inc=True)
                    p.cp("act", osb[:, t0:t0 + n], pa[:, 0:n], [bpa], [Bo])
                    p.tt("pool", osq[:, 0:n], osb[:, t0:t0 + n], osb[:, t0:t0 + n], ALU.mult, [Bo], [Bosq])
                    pm_ = pA[ai % 2]
                    bpm_ = BpA[ai % 2]
                    ai += 1
                    p.mm(pm_[:, 0:n], onesF[:, :], osb[:, t0:t0 + n], True, True, [Bo, BC], [bpm_])
                    p.cp("act", mu[:, 0:n], pm_[:, 0:n], [bpm_], [Bmu])
                    pv_ = pA[ai % 2]
                    bpv_ = BpA[ai % 2]
                    ai += 1
                    p.mm(pv_[:, 0:n], onesF[:, :], osq[:, 0:n], True, True, [Bosq, BC], [bpv_])
                    p.tt("pool", osq[:, 0:n], mu[:, 0:n], mu[:, 0:n], ALU.mult, [Bmu], [Bosq])
                    p.tt("dve", var[:, 0:n], pv_[:, 0:n], osq[:, 0:n], ALU.subtract, [bpv_, Bosq], [Bvar])
                    p.act(var[:, 0:n], var[:, 0:n], AF.Sqrt, [Bvar, BC], [Bvar], bias=epsT[:, 0:1], scale=1.0)
                    p.op("dve", lambda e, o=var[:, 0:n]: e.reciprocal(out=o, in_=o), reads=[Bvar], writes=[Bvar])
                    p.tt("pool", mu[:, 0:n], osb[:, t0:t0 + n], mu[:, 0:n], ALU.subtract, [Bo, Bmu], [Bmu])
                    p.tt("dve", mu[:, 0:n], mu[:, 0:n], var[:, 0:n], ALU.mult, [Bmu, Bvar], [Bmu])
                    p.tt("dve", yo[:, t0:t0 + n], mu[:, 0:n], gT[:, t0:t0 + n], ALU.mult, [Bmu, Bg], [Byo])
                p.dma("sp", yT_d[1536 + h * 128:1536 + (h + 1) * 128, :], yo[:], reads=[Byo], writes=[B_yF])

    def phase_G(l, ctx=None):
        last = l == nl - 1
        with (ctx or OwnCtx()) as (es, pes):
            wo = es.enter_context(SBT("g_wo", [128, 16, D], BF16))
            Bw = Buf()
            for q4 in range(4):
                p.dma("pool", wo[:, :, q4 * 512:(q4 + 1) * 512],
                      I["w_out"][l, :, q4 * 512:(q4 + 1) * 512].rearrange("(k p) c -> p k c", p=128), writes=[Bw])
            yb = [es.enter_context(SBT(f"g_y{i}", [128, 16, 128], BF16)) for i in range(2)]
            Byb = [Buf(), Buf()]
            ht = [es.enter_context(SBT(f"g_h{i}", [128, D], F32)) for i in range(2)]
            Bht = [Buf(), Buf()]
            tmp = [es.enter_context(SBT(f"g_t{i}", [128, D], F32)) for i in range(2)]
            Btmp = [Buf(), Buf()]
            st = es.enter_context(SBT("g_st", [128, 2, 4, 6], F32))
            mv = es.enter_context(SBT("g_mv", [128, 2, 4], F32))
            Bst = [Buf(), Buf()]
            pz = [pes.enter_context(PST(f"g_pz{i}", [128, 512], F32)) for i in range(8)]
            Bpz = [Buf() for _ in range(8)]
            tiles = list(range(2, 18)) if last else list(range(18))
            def g_load(it):
                t_ = tiles[it]
                b_ = it % 2
                tk = slice(t_ * 128, (t_ + 1) * 128)
                p.dma("sp", yb[b_][:], yT_d[:, tk].rearrange("(k p) t -> p k t", p=128), reads=[B_yD, B_yE, B_yF], writes=[Byb[b_]])
                p.dma("sp", ht[b_][:], h_d[tk, :], reads=[B_h], writes=[Bht[b_]])
            g_load(0)
            for it, t in enumerate(tiles):
                b2 = it % 2
                r = 1 if t < 2 else 0
                tok = slice(t * 128, (t + 1) * 128)
                if it + 1 < len(tiles):
                    g_load(it + 1)
                for q4 in range(4):
                    pz_ = pz[b2 * 4 + q4]
                    bz_ = Bpz[b2 * 4 + q4]
                    for k in range(16):
                        p.mm(pz_[:, :], yb[b2][:, k, :], wo[:, k, q4 * 512:(q4 + 1) * 512], k == 0, k == 15, [Byb[b2], Bw], [bz_])
                    cs = slice(q4 * 512, (q4 + 1) * 512)
                    p.tt("dve", tmp[b2][:, cs], pz_[:, :], gt_bc[:, r, cs], ALU.mult, [bz_, Bmod], [Btmp[b2]])
                    p.stt("dve", tmp[b2][:, cs], ht[b2][:, cs], ALPHA, tmp[b2][:, cs], ALU.mult, ALU.add, [Bht[b2], Btmp[b2]], [Btmp[b2]])
                    p.op("dve", lambda e, o=st[:, b2, q4, :], i_=tmp[b2][:, cs]: e.bn_stats(out=o, in_=i_),
                         reads=[Btmp[b2]], writes=[Bst[b2]])
                p.op("dve", lambda e, o=mv[:, b2, 0:2], i_=st[:, b2, :, :].rearrange("p c s -> p (c s)"): e.bn_aggr(out=o, in_=i_), reads=[Bst[b2]], writes=[Bst[b2]])
                p.act(mv[:, b2, 2:3], mv[:, b2, 1:2], AF.Sqrt, [Bst[b2], BC], [Bst[b2]], bias=epsT[:, 0:1], scale=1.0)
                p.op("dve", lambda e, o=mv[:, b2, 2:3]: e.reciprocal(out=o, in_=o), reads=[Bst[b2]], writes=[Bst[b2]])
                p.stt("dve", mv[:, b2, 3:4], mv[:, b2, 0:1], -1.0, mv[:, b2, 2:3], ALU.mult, ALU.mult, [Bst[b2]], [Bst[b2]])
                p.act(tmp[b2][:], tmp[b2][:], AF.Identity, [Bst[b2], Btmp[b2]], [Btmp[b2]], bias=mv[:, b2, 3:4], scale=mv[:, b2, 2:3])
                p.tt("pool", tmp[b2][:], tmp[b2][:], lng_bc[:], ALU.mult, [Btmp[b2], Bmod], [Btmp[b2]])
                p.tt("dve", ht[b2][:], tmp[b2][:], lnb_bc[:], ALU.add, [Btmp[b2], Bmod, Bht[b2]], [Bht[b2]])
                if last:
                    p.dma("sp", out_d[(t - 2) * 128:(t - 1) * 128, :], ht[b2][:], reads=[Bht[b2]])
                else:
                    p.dma("sp", h_d[tok, :], ht[b2][:], reads=[Bht[b2]], writes=[B_h])

    done = False
    for l in range(nl):
        steps = [("ada", lambda: phase_ada(l)), ("AB", lambda: phase_AB(l)), ("C", lambda: phase_C(l)),
                 ("DE", lambda: run_concurrent(l, [(1, phase_D), (2, phase_E)])),
                 ("F", lambda: phase_F(l)), ("G", lambda: phase_G(l))]
        for nm, fn in steps:
            fn()
            if stop_after == (l, nm):
                done = True
                break
        if done:
            break
    p.final_dma_wait()
    p.flush()
    p.stack.close()
    return nc


_CACHE = {}
NCORES = 8


def make_in_maps(inputs, consts):
    f = lambda a: np.ascontiguousarray(np.asarray(a, dtype=np.float32))
    shared = {}
    for n_, shp in W_NAMES:
        shared[n_] = f(inputs[n_]).reshape(shp)
    for n_, shp in C_NAMES:
        shared[n_] = f(consts[n_]).reshape(shp)
    x = f(inputs["x"])
    ctx = f(inputs["ctx"])
    c = f(inputs["c"])
    cc = f(inputs["c_ctx"])
    maps = []
    for core in range(NCORES):
        b = core % 4
        m = dict(shared)
        m["x"] = x[b]
        m["ctx"] = ctx[b]
        m["c2"] = np.ascontiguousarray(np.stack([c[b], cc], axis=0))
        maps.append(m)
    return maps


def kernel(**inputs):
    if "nc" not in _CACHE:
        _CACHE["nc"] = build()
    nc = _CACHE["nc"]
    maps = make_in_maps(inputs, host_consts())
    res = run_bass_kernel_spmd(nc, maps, core_ids=list(range(NCORES)))
    out = np.stack([np.asarray(res.results[b]["out"], dtype=np.float32) for b in range(4)], axis=0)
    return out
```
